# Optimizing a Trainium2 kernel written in Bass

```python
import math
import jax, jax.numpy as jnp
from jax import lax
import numpy as np


D_MODEL = 1024
BATCH = 4
SEQ = 4096
DEPTH = 1
DEC_BATCH = 128
DEC_SEQ = 1
PAST_LEN = 8192
PAGE_SIZE = 128

HEAD_DIM = 64
N_HEADS = D_MODEL // HEAD_DIM
KV_HEADS = N_HEADS // 4
GROUP = N_HEADS // KV_HEADS
Q_W = N_HEADS * HEAD_DIM
KV_W = KV_HEADS * HEAD_DIM
WINDOW = 128
BLK = WINDOW
N_BUCKETS = 32
MAX_EXACT = N_BUCKETS // 2
MAX_DIST = WINDOW
ATT_SCALE = HEAD_DIM ** -0.5
RNN_W = D_MODEL
N_BLOCKS = 16
BLOCK_W = RNN_W // N_BLOCKS
CONV_W = 4
LRU_C = 8.0
EPS = 1e-6
NEG = -1e30
SPLITS = (Q_W, KV_W, KV_W, Q_W, RNN_W, RNN_W, D_MODEL, D_MODEL)
IN_W = sum(SPLITS)

kernel_name = 'hybrid_swa_sink_rglru_gated_merge_step'


def _rmsnorm(x, g):
    xf = x.astype(jnp.float32)
    y = xf * lax.rsqrt(jnp.mean(xf * xf, axis=-1, keepdims=True) + EPS) * g.astype(jnp.float32)
    return y.astype(x.dtype)


def _rel_bucket(dist):
    n = jnp.maximum(dist, 0)
    nf = jnp.maximum(n, 1).astype(jnp.float32)
    large = MAX_EXACT + (jnp.log(nf / MAX_EXACT) / math.log(MAX_DIST / MAX_EXACT)
                         * (N_BUCKETS - MAX_EXACT)).astype(jnp.int32)
    large = jnp.minimum(large, N_BUCKETS - 1)
    return jnp.where(n < MAX_EXACT, n, large)


def _rel_bias(rel_table, dist):
    b = rel_table[_rel_bucket(dist)].astype(jnp.float32)
    return jnp.transpose(b, (2, 0, 1)).reshape(KV_HEADS, GROUP, dist.shape[0], dist.shape[1])


def _sink_softmax(logits, sinks):
    s = jnp.broadcast_to(sinks.astype(jnp.float32).reshape(KV_HEADS, GROUP, 1, 1),
                         logits.shape[:-1] + (1,))
    p = jax.nn.softmax(jnp.concatenate([logits, s], axis=-1), axis=-1)
    return p[..., :-1]


def _band_attention(q, k, v, rel_table, sinks):
    B, S = q.shape[0], q.shape[1]
    nb = S // BLK
    qb = q.reshape(B, nb, BLK, KV_HEADS, GROUP, HEAD_DIM).astype(jnp.float32)
    pad = jnp.zeros((B, BLK, KV_HEADS, HEAD_DIM), k.dtype)
    kp = jnp.concatenate([pad, k], axis=1)
    vp = jnp.concatenate([pad, v], axis=1)
    shp = (B, nb, BLK, KV_HEADS, HEAD_DIM)
    kb = jnp.concatenate([kp[:, :S].reshape(shp), kp[:, BLK:].reshape(shp)], axis=2).astype(jnp.float32)
    vb = jnp.concatenate([vp[:, :S].reshape(shp), vp[:, BLK:].reshape(shp)], axis=2).astype(jnp.float32)
    logits = jnp.einsum('bnqkgd,bnskd->bnkgqs', qb, kb) * ATT_SCALE
    qi = jnp.arange(BLK)[:, None]
    kj = jnp.arange(2 * BLK)[None, :]
    dist = qi + BLK - kj
    band = (dist >= 0) & (dist < WINDOW)
    not_pad = (jnp.arange(nb)[:, None, None] > 0) | (kj >= BLK)[None]
    valid = band[None] & not_pad
    logits = jnp.where(valid[None, :, None, None], logits + _rel_bias(rel_table, dist), NEG)
    p = _sink_softmax(logits, sinks)
    o = jnp.einsum('bnkgqs,bnskd->bnqkgd', p, vb)
    return o.reshape(B, S, Q_W).astype(q.dtype)


def _window_step_attention(q, k, v, k_past, v_past, rel_table, sinks):
    B, T = q.shape[0], q.shape[1]
    W = k_past.shape[1]
    kc = jnp.concatenate([k_past, k.astype(k_past.dtype)], axis=1)
    vc = jnp.concatenate([v_past, v.astype(v_past.dtype)], axis=1)
    qg = q.reshape(B, T, KV_HEADS, GROUP, HEAD_DIM).astype(jnp.float32)
    logits = jnp.einsum('btkgd,bskd->bkgts', qg, kc.astype(jnp.float32)) * ATT_SCALE
    kpos = jnp.concatenate([jnp.arange(W) - W, jnp.arange(T)])
    dist = jnp.arange(T)[:, None] - kpos[None, :]
    valid = (dist >= 0) & (dist < WINDOW)
    logits = jnp.where(valid, logits + _rel_bias(rel_table, dist), NEG)
    p = _sink_softmax(logits, sinks)
    o = jnp.einsum('bkgts,bskd->btkgd', p, vc.astype(jnp.float32)).reshape(B, T, Q_W)
    return o.astype(q.dtype), kc[:, -W:], vc[:, -W:]


def _causal_conv(xr, buf, w_conv, b_conv):
    T = xr.shape[1]
    xp = jnp.concatenate([buf.astype(xr.dtype), xr], axis=1)
    y = b_conv
    for j in range(CONV_W):
        y = y + xp[:, j:j + T] * w_conv[j]
    return y, xp[:, -(CONV_W - 1):]


def _rglru(xc, h0, w_ra, b_ra, w_rx, b_rx, lam):
    B, T = xc.shape[0], xc.shape[1]
    xf = xc.astype(jnp.float32)
    xb = xf.reshape(B, T, N_BLOCKS, BLOCK_W)
    r = jax.nn.sigmoid(jnp.einsum('btnc,ncd->btnd', xb, w_ra.astype(jnp.float32)).reshape(B, T, RNN_W)
                       + b_ra.astype(jnp.float32))
    i = jax.nn.sigmoid(jnp.einsum('btnc,ncd->btnd', xb, w_rx.astype(jnp.float32)).reshape(B, T, RNN_W)
                       + b_rx.astype(jnp.float32))
    log_a = -LRU_C * r * jax.nn.softplus(-lam.astype(jnp.float32))
    a = jnp.exp(log_a)
    u = jnp.sqrt(-jnp.expm1(2.0 * log_a)) * (i * xf)

    def step(h, au):
        a_t, u_t = au
        h = a_t * h + u_t
        return h, h

    h_last, hs = lax.scan(step, h0.astype(jnp.float32), (jnp.swapaxes(a, 0, 1), jnp.swapaxes(u, 0, 1)))
    return jnp.swapaxes(hs, 0, 1).astype(xc.dtype), h_last.astype(xc.dtype)


def _layer(x, c, k_past, v_past, conv_past, h_past, rel_table, w, prompt):
    (norm_g, w_ada, b_ada, w_in, q_g, k_g, sinks, w_conv, b_conv,
     w_ra, b_ra, w_rx, b_rx, lam, w_pa, w_pr, w_out) = w
    B, T = x.shape[0], x.shape[1]
    mod = (c @ w_ada + b_ada)[:, None, :]
    shift, scale, gate = jnp.split(mod, 3, axis=-1)
    h = _rmsnorm(x, norm_g) * (1.0 + scale) + shift
    z = h @ w_in
    offs = np.cumsum(SPLITS)[:-1].tolist()
    q, k, v, ga, xr, gr, ma, mr = jnp.split(z, offs, axis=-1)
    q = _rmsnorm(q.reshape(B, T, N_HEADS, HEAD_DIM), q_g)
    k = _rmsnorm(k.reshape(B, T, KV_HEADS, HEAD_DIM), k_g)
    v = v.reshape(B, T, KV_HEADS, HEAD_DIM)
    if prompt:
        o = _band_attention(q, k, v, rel_table, sinks)
        wp = min(WINDOW, T)
        k_new, v_new = k[:, -wp:], v[:, -wp:]
        conv_buf = jnp.zeros((B, CONV_W - 1, RNN_W), x.dtype)
        h0 = jnp.zeros((B, RNN_W), x.dtype)
    else:
        o, k_new, v_new = _window_step_attention(q, k, v, k_past, v_past, rel_table, sinks)
        conv_buf, h0 = conv_past, h_past
    ya = o * jax.nn.silu(ga)
    xc, conv_new = _causal_conv(xr, conv_buf, w_conv, b_conv)
    hs, h_last = _rglru(xc, h0, w_ra, b_ra, w_rx, b_rx, lam)
    yr = hs * jax.nn.silu(gr)
    merged = jax.nn.sigmoid(ma) * (ya @ w_pa) + jax.nn.sigmoid(mr) * (yr @ w_pr)
    y = x + gate * (merged @ w_out)
    return y, k_new, v_new, conv_new, h_last


def setup_inputs(seed: int = 0) -> dict:
    key = jax.random.key(seed)
    ks = jax.random.split(key, 28)
    f32 = jnp.float32
    nrm = lambda k, s, sc: jax.random.normal(k, s, f32) * sc
    W = min(WINDOW, PAST_LEN)
    a8 = jax.random.uniform(ks[22], (DEPTH, RNN_W), f32, minval=0.9, maxval=0.999)
    s = a8 ** (1.0 / LRU_C)
    lam = jnp.log(s) - jnp.log1p(-s)
    return {
        'x_prompt': nrm(ks[0], (BATCH, SEQ, D_MODEL), 1.0),
        'x_sample': nrm(ks[1], (DEC_BATCH, DEC_SEQ, D_MODEL), 1.0),
        'cache_k': nrm(ks[2], (DEPTH, DEC_BATCH, W, KV_HEADS, HEAD_DIM), 1.0),
        'cache_v': nrm(ks[3], (DEPTH, DEC_BATCH, W, KV_HEADS, HEAD_DIM), 1.0),
        'state_conv': nrm(ks[4], (DEPTH, DEC_BATCH, CONV_W - 1, RNN_W), 1.0),
        'state_h': nrm(ks[5], (DEPTH, DEC_BATCH, RNN_W), 0.5),
        'c_prompt': nrm(ks[6], (BATCH, D_MODEL), 1.0),
        'c_sample': nrm(ks[7], (DEC_BATCH, D_MODEL), 1.0),
        'rel_table': nrm(ks[8], (N_BUCKETS, N_HEADS), 0.5),
        'norm_g': 1.0 + nrm(ks[9], (DEPTH, D_MODEL), 0.02),
        'w_ada': nrm(ks[10], (DEPTH, D_MODEL, 3 * D_MODEL), 0.5 * D_MODEL ** -0.5),
        'b_ada': nrm(ks[11], (DEPTH, 3 * D_MODEL), 0.01),
        'w_in': nrm(ks[12], (DEPTH, D_MODEL, IN_W), D_MODEL ** -0.5),
        'q_norm_g': 1.0 + nrm(ks[13], (DEPTH, HEAD_DIM), 0.02),
        'k_norm_g': 1.0 + nrm(ks[14], (DEPTH, HEAD_DIM), 0.02),
        'sinks': nrm(ks[15], (DEPTH, N_HEADS), 0.5),
        'w_conv': nrm(ks[16], (DEPTH, CONV_W, RNN_W), CONV_W ** -0.5),
        'b_conv': nrm(ks[17], (DEPTH, RNN_W), 0.01),
        'w_rg_a': nrm(ks[18], (DEPTH, N_BLOCKS, BLOCK_W, BLOCK_W), BLOCK_W ** -0.5),
        'b_rg_a': nrm(ks[19], (DEPTH, RNN_W), 0.01),
        'w_rg_x': nrm(ks[20], (DEPTH, N_BLOCKS, BLOCK_W, BLOCK_W), BLOCK_W ** -0.5),
        'b_rg_x': nrm(ks[21], (DEPTH, RNN_W), 0.01),
        'lru_lambda': lam,
        'w_proj_attn': nrm(ks[23], (DEPTH, Q_W, D_MODEL), Q_W ** -0.5),
        'w_proj_rnn': nrm(ks[24], (DEPTH, RNN_W, D_MODEL), RNN_W ** -0.5),
        'w_out': nrm(ks[25], (DEPTH, D_MODEL, D_MODEL), D_MODEL ** -0.5),
    }


def reference(x_prompt, x_sample, cache_k, cache_v, state_conv, state_h, c_prompt, c_sample,
              rel_table, norm_g, w_ada, b_ada, w_in, q_norm_g, k_norm_g, sinks, w_conv, b_conv,
              w_rg_a, b_rg_a, w_rg_x, b_rg_x, lru_lambda, w_proj_attn, w_proj_rnn, w_out):
    y_p, y_s = x_prompt, x_sample
    kp_l, vp_l, cp_l, hp_l = [], [], [], []
    ks_l, vs_l, cs_l, hs_l = [], [], [], []
    for l in range(DEPTH):
        w = (norm_g[l], w_ada[l], b_ada[l], w_in[l], q_norm_g[l], k_norm_g[l], sinks[l],
             w_conv[l], b_conv[l], w_rg_a[l], b_rg_a[l], w_rg_x[l], b_rg_x[l], lru_lambda[l],
             w_proj_attn[l], w_proj_rnn[l], w_out[l])
        y_p, kp, vp, cp, hp = _layer(y_p, c_prompt, None, None, None, None, rel_table, w, True)
        y_s, kk, vv, cc, hh = _layer(y_s, c_sample, cache_k[l], cache_v[l], state_conv[l],
                                     state_h[l], rel_table, w, False)
        kp_l.append(kp); vp_l.append(vp); cp_l.append(cp); hp_l.append(hp)
        ks_l.append(kk); vs_l.append(vv); cs_l.append(cc); hs_l.append(hh)
    k_win_prompt = jnp.stack(kp_l)
    v_win_prompt = jnp.stack(vp_l)
    conv_prompt = jnp.stack(cp_l)
    h_prompt = jnp.stack(hp_l)
    k_win_sample = jnp.stack(ks_l)
    v_win_sample = jnp.stack(vs_l)
    conv_sample = jnp.stack(cs_l)
    h_sample = jnp.stack(hs_l)
    return (y_p, y_s, k_win_prompt, v_win_prompt, conv_prompt, h_prompt,
            k_win_sample, v_win_sample, conv_sample, h_sample)
```

```python
import numpy as np
import concourse.bass as bass
import concourse.mybir as mybir
from concourse.bass_utils import run_bass_kernel_spmd

F32 = mybir.dt.float32
BF16 = mybir.dt.bfloat16
AF = mybir.ActivationFunctionType
ALU = mybir.AluOpType
DSZ = {F32: 4, BF16: 2}

D = 1024
NTOK = 2048
NS = 16
TW = NTOK + NS
INW = 6656
EPS = 1e-6
OFF_Q, OFF_K, OFF_V, OFF_GA, OFF_XR, OFF_GR, OFF_MA, OFF_MR = 0, 1024, 1280, 1536, 2560, 3584, 4608, 5632
TILES = [(0, 512), (512, 512), (1024, 512), (1536, 512), (2048, 16)]
NEGB = -3750.0

PV = {}
_c = 0
for _n, _w in [("ng", 8), ("bsh", 8), ("bsc", 8), ("wc0", 8), ("wc1", 8), ("wc2", 8), ("wc3", 8), ("bcv", 8),
               ("bra", 8), ("brx", 8), ("lam", 8), ("qg", 1), ("kg", 1), ("snk", 8), ("cp", 8), ("pv", 1),
               ("pm", 1)]:
    PV[_n] = (_c, _w)
    _c += _w
NPV = _c


class Prog:
    def __init__(self, nc):
        self.nc = nc
        self.ops = []
        self.track = {}
        self.stopped = False
        self.kstop = ""

    def mark(self, name):
        if self.kstop == name:
            self.stopped = True

    @staticmethod
    def region(ap):
        t = ap.tensor
        name = t.name
        esz = DSZ.get(ap.dtype, 4)
        dims = list(ap.ap)
        off = ap.offset
        if type(t).__name__ not in ("SBTensorHandle", "PSumTensorHandle"):
            hi = off + sum((c - 1) * abs(st) for st, c in dims) + 1
            return (name, 0, 1, off * esz, hi * esz)
        if type(t).__name__ == "PSumTensorHandle":
            return (name, 0, 128, 0, 1 << 20)
        psz = 1
        for d_ in list(t.shape)[1:]:
            psz *= d_
        p0 = off // psz
        f0 = off % psz
        pcnt = dims[0][1]
        fhi = f0 + sum((c - 1) * abs(st) for st, c in dims[1:]) + 1
        return (name, p0, p0 + pcnt, f0 * esz, fhi * esz)

    @staticmethod
    def overlap(a, b):
        return a[1] < b[2] and b[1] < a[2] and a[3] < b[4] and b[3] < a[4]

    @staticmethod
    def contains(a, b):
        return a[1] <= b[1] and b[2] <= a[2] and a[3] <= b[3] and b[4] <= a[4]

    def add(self, eng, emit, reads=(), writes=(), dma=False, out=False):
        if self.stopped:
            return -1
        oid = len(self.ops)
        deps = set()
        rregs = [self.region(a) for a in reads]
        wregs = [self.region(a) for a in writes]
        wregs = wregs + [r for r in rregs if r[4] == (1 << 20)]
        rregs = [r for r in rregs if r[4] != (1 << 20)]
        for r in rregs:
            tr = self.track.setdefault(r[0], {"w": [], "r": []})
            for (reg, o) in tr["w"]:
                if self.overlap(reg, r):
                    deps.add(o)
        for w in wregs:
            tr = self.track.setdefault(w[0], {"w": [], "r": []})
            for (reg, o) in tr["w"]:
                if self.overlap(reg, w):
                    deps.add(o)
            for (reg, o) in tr["r"]:
                if self.overlap(reg, w):
                    deps.add(o)
        for r in rregs:
            self.track[r[0]]["r"].append((r, oid))
        for w in wregs:
            tr = self.track[w[0]]
            tr["w"] = [(reg, o) for (reg, o) in tr["w"] if not self.contains(w, reg)]
            tr["r"] = [(reg, o) for (reg, o) in tr["r"] if not self.contains(w, reg) or o == oid]
            tr["w"].append((w, oid))
        deps.discard(oid)
        self.ops.append(dict(eng=eng, emit=emit, deps=deps, dma=dma, out=out, tok=None, prev=None))
        return oid

    def emit_all(self, sems, dma_sems):
        ops = self.ops
        needed = set()
        for op in ops:
            needed |= op["deps"]
        cnt = {e: 0 for e in sems}
        dcnt = {}
        dk = {e: 0 for e in dma_sems}
        for i, op in enumerate(ops):
            e = op["eng"]
            if op["dma"]:
                pool = dma_sems[e]
                s = pool[dk[e] % len(pool)]
                dk[e] += 1
                prev = dcnt.get(s, 0)
                dcnt[s] = prev + 16
                op["tok"] = (s, prev + 16)
                op["prev"] = (s, prev)
            elif i in needed:
                cnt[e] += 1
                op["tok"] = (sems[e], cnt[e])
        by_eng = {}
        for i, op in enumerate(ops):
            by_eng.setdefault(op["eng"], []).append(i)

        def run(ename, eh):
            seen = {}
            for i in by_eng.get(ename, []):
                op = ops[i]
                waits = {}
                for d in op["deps"]:
                    dop = ops[d]
                    if ename == "pe" and dop["eng"] == "pe" and not dop["dma"]:
                        continue
                    s, v = dop["tok"]
                    waits[s] = max(waits.get(s, 0), v)
                if op["dma"] and op["prev"][1] > 0:
                    s, v = op["prev"]
                    waits[s] = max(waits.get(s, 0), v)
                for s, v in waits.items():
                    if seen.get(s, 0) < v:
                        eh.wait_ge(s, v)
                        seen[s] = v
                ins = op["emit"](eh)
                if op["tok"] is not None:
                    ins.then_inc(op["tok"][0], 16 if op["dma"] else 1)
            if ename == "sp":
                for s, v in dcnt.items():
                    if seen.get(s, 0) < v:
                        eh.wait_ge(s, v)
        return run


def build_nc():
    nc = bass.Bass("TRN2", target_bir_lowering=False)
    P = Prog(nc)

    def din(name, shape, dt=F32):
        return nc.dram_tensor(name, list(shape), dt, kind="ExternalInput").ap()

    def dout(name, shape, dt=F32):
        return nc.dram_tensor(name, list(shape), dt, kind="ExternalOutput").ap()

    xp_d = din("xp", [NTOK, D])
    xq_d = din("xq", [NTOK, D])
    xs_d = din("xs", [NS, D])
    ck_d = din("ck", [NS, 128, 256])
    cv_d = din("cv", [NS, 128, 256])
    scT_d = din("scT", [128, 8, 3, NS])
    shT_d = din("shT", [128, 8, NS])
    sc12_d = din("sc12", [NS, 2, D])
    pvec_d = din("pvec", [128, NPV])
    csT_d = din("csT", [128, 8, NS])
    bgate_d = din("bgate", [1, D])
    kgbc_d = din("kgbc", [128, 64])
    rel_d = din("relx", [33, 16])
    ohe_d = din("ohe", [33, 383])
    cst_d = din("cst", [128, 3, 128])
    wada_d = din("w_ada", [D, 3 * D])
    win_d = din("w_in", [D, INW])
    wra_d = din("wra", [128, 8, 128])
    wrx_d = din("wrx", [128, 8, 128])
    wpa_d = din("w_pa", [D, D])
    wpr_d = din("w_pr", [D, D])
    wout_d = din("w_out", [D, D])

    y_d = dout("y", [NTOK, D])
    ys_d = dout("ys", [NS, D])
    kwp_d = dout("kwp", [128, 256])
    vwp_d = dout("vwp", [128, 256])
    cvp_d = dout("cvp", [128, 8, 3])
    hp_d = dout("hp", [128, 8])
    kws_d = dout("kws", [NS, 128, 256])
    vws_d = dout("vws", [NS, 128, 256])
    cvs_d = dout("cvs", [128, 8, NS])
    cvs12_d = dout("cvs12", [NS, 2, D])
    hsm_d = dout("hsm", [128, 8, NS])
    ext_d = nc.dram_tensor("ext_scr", [16, 383], F32, kind="Internal").ap()
    knew_d = nc.dram_tensor("knew_scr", [NS, 256], F32, kind="Internal").ap()
    vnew_d = nc.dram_tensor("vnew_scr", [NS, 256], F32, kind="Internal").ap()

    from contextlib import ExitStack
    es = ExitStack()

    def sb(name, shape, dt=F32):
        return es.enter_context(nc.sbuf_tensor("sb_" + name, list(shape), dt))

    def ps(name, shape, dt=F32):
        return es.enter_context(nc.psum_tensor("ps_" + name, list(shape), dt))

    with es:
        hT = sb("hT", [128, 8, TW], BF16)
        YA = sb("YA", [128, 8, TW], BF16)
        KV = sb("KV", [128, 17920], BF16)
        KT = KV[:, 0:4 * 2192].rearrange("p (c n) -> p c n", c=4)
        Vb = KV[:, 8768:8768 + 17 * 512].rearrange("p (b g e) -> p b g e", b=17, g=4)
        YR = KV[:, 0:8 * TW].rearrange("p (c n) -> p c n", c=8)
        WB = sb("WB", [128, 2, 8, 512], BF16)
        WS = sb("WS", [128, 4, 8, 128], BF16)
        XT = sb("XT", [128, 2, D], F32)
        XN = sb("XN", [128, 2, D], BF16)
        TMP = sb("TMP", [128, 8, 512], F32)
        TMPB = sb("TMPB", [128, 4, 512], BF16)
        LOCb = sb("LOCb", [128, 8320], BF16)
        DG = LOCb[:, 0:4096].rearrange("p (c j m) -> p c j m", c=8, j=4)
        XRb = LOCb[:, 4096:4096 + 4160].rearrange("p (b n) -> p b n", b=2)
        B8 = LOCb[:, 0:4096].rearrange("p (h q) -> p h q", h=16)
        MGt = LOCb[:, 0:4096].rearrange("p (e n) -> p e n", e=8)
        GATES = sb("GATES", [128, 2080], F32)
        GBC = GATES[:, 0:1024]
        GS = GATES[0:NS, 1024:2048]
        SG = GATES[:, 0:TW]
        TK = GATES[:, 0:1024].rearrange("p (a f) -> p a f", a=4)
        ROt = sb("ROt", [128, 1040], F32)
        QTF = sb("QTF", [128, 1040], F32)
        QT = QTF[:, :].bitcast(BF16)[:, 0:TW]
        KVf = KV[:, :].bitcast(F32)
        RARR_P = [(KVf[:, 0:1040], KVf[:, 1040:2080]), (KVf[:, 2080:3120], KVf[:, 3120:4160])]
        RARR_R = [(GATES[:, 0:1040], GATES[:, 1040:2080]), (ROt[:, :], QTF[:, :])]
        EB = sb("EB", [128, 3, 2, 256], BF16)
        pvec = sb("pvec", [128, NPV], F32)
        cstf = XT[:, 1, 0:384].rearrange("p (a m) -> p a m", a=3)
        cstb = sb("cstb", [128, 3, 128], BF16)
        identF = sb("identF", [128, 128], F32)
        ones_b = sb("ones_b", [128, 128], BF16)
        small = sb("small", [128, 128], F32)
        cT17 = sb("cT17", [128, 8, 17], BF16)
        csTf = sb("csTf", [128, 8, NS], F32)
        modS = sb("modS", [128, 16, 17], F32)
        wrab = sb("wrab", [128, 2, 8, 128], BF16)
        kgbc = sb("kgbc", [128, 64], F32)
        rowb = XN[0:1, 1, :]
        relf = XT[0:33, 0, 0:400]
        relbt = sb("relb", [33, 400], BF16)
        relb = relbt[:, :]
        BS = sb("BS", [128, 16], F32)
        scT = sb("scT", [128, 8, 3, NS], F32)
        shT = sb("shT", [128, 8, NS], F32)
        HSs = sb("HSs", [128, 8, NS], F32)
        XRs = sb("XRs", [128, 8, NS], F32)
        HL = sb("HL", [128, 16], F32)
        XTL = sb("XTL", [128, 8, 3], F32)
        CVP = sb("CVP", [128, 8, 3], F32)
        SGS = sb("SGS", [128, 8, NS], F32)
        KTs = sb("KTs", [128, 2, 128], BF16)
        ESs = sb("ESs", [128, NS, 16], BF16)
        DSs = sb("DSs", [128, 8, NS], F32)
        SSQ = sb("SSQ", [128, 40], F32)
        QTs = sb("QTs", [128, 8, NS], BF16)

        banks = [ps(f"bk{i}", [128, 512], F32) for i in range(8)]

        ident = cstb[:, 0, :]
        Jm = cstb[:, 1, :]
        bones = cstb[:, 2, :]

        def pv(name, c=0, w=1):
            o, _ = PV[name]
            return pvec[:, o + c:o + c + w]

        _sm = [0]

        def smalloc(w):
            o = _sm[0]
            _sm[0] += w
            assert _sm[0] <= 128
            return small[:, o:o + w]

        def DMA(eng, out, in_, is_out=False):
            P.add(eng, lambda e: e.dma_start(out=out, in_=in_), reads=[in_], writes=[out], dma=True, out=is_out)

        def MM(out, lhsT, rhs, start=True, stop=True, tp=None):
            if tp is None:
                P.add("pe", lambda e: e.matmul(out, lhsT=lhsT, rhs=rhs, start=start, stop=stop),
                      reads=[lhsT, rhs], writes=[out])
            else:
                P.add("pe", lambda e: e.matmul(out, lhsT=lhsT, rhs=rhs, start=start, stop=stop, tile_position=tp),
                      reads=[lhsT, rhs], writes=[out])

        def TR(out, in_, rows=128):
            idn = ident[0:rows, 0:rows]
            P.add("pe", lambda e: e.transpose(out=out, in_=in_, identity=idn), reads=[in_, idn], writes=[out])

        def ACT(out, in_, func, scale=1.0, bias=None, accum=None):
            rd = [in_]
            kw = {}
            if isinstance(scale, float) or isinstance(scale, int):
                kw["scale"] = float(scale)
            else:
                kw["scale"] = scale
                rd.append(scale)
            if bias is not None:
                kw["bias"] = bias
                if not isinstance(bias, float):
                    rd.append(bias)
            wr = [out]
            if accum is not None:
                kw["accum_out"] = accum
                wr.append(accum)
            P.add("act", lambda e: e.activation(out=out, in_=in_, func=func, **kw), reads=rd, writes=wr)

        def TS(out, in0, s1, s2, op0, op1=None, eng="dve"):
            if op1 is None:
                rd = [in0] + ([] if isinstance(s1, (float, int)) else [s1])
                P.add(eng, lambda e: e.tensor_scalar(out=out, in0=in0, scalar1=s1, scalar2=None, op0=op0),
                      reads=rd, writes=[out])
                return
            assert isinstance(s1, (float, int)) == isinstance(s2, (float, int)), "mixed AP/imm scalars"
            rd = [in0] + [s_ for s_ in (s1, s2) if not isinstance(s_, (float, int))]
            P.add(eng, lambda e: e.tensor_scalar(out=out, in0=in0, scalar1=s1, scalar2=s2, op0=op0, op1=op1),
                  reads=rd, writes=[out])

        def TT(out, in0, in1, op, eng="dve"):
            P.add(eng, lambda e: e.tensor_tensor(out=out, in0=in0, in1=in1, op=op), reads=[in0, in1], writes=[out])

        def STT(out, in0, scalar, in1, op0, op1, eng="dve"):
            rd = [in0, in1] + ([] if isinstance(scalar, (float, int)) else [scalar])
            P.add(eng, lambda e: e.scalar_tensor_tensor(out=out, in0=in0, scalar=scalar, in1=in1, op0=op0, op1=op1),
                  reads=rd, writes=[out])

        def CP(out, in_, eng="dve"):
            P.add(eng, lambda e: e.tensor_copy(out=out, in_=in_), reads=[in_], writes=[out])

        def MSET(ap, val, eng="dve"):
            P.add(eng, lambda e: e.memset(ap, val), reads=[], writes=[ap])

        def RECIP(out, in_):
            P.add("dve", lambda e: e.reciprocal(out=out, in_=in_), reads=[in_], writes=[out])

        def SCAN(out, d0, d1, init):
            rd = [d0, d1] + ([] if isinstance(init, (float, int)) else [init])
            P.add("dve", lambda e: e.tensor_tensor_scan(out=out, data0=d0, data1=d1, initial=init,
                                                         op0=ALU.mult, op1=ALU.add), reads=rd, writes=[out])

        _bk = [0]
        _pool = [0, 1, 2, 3]

        def bank():
            b = banks[_pool[_bk[0] % len(_pool)]]
            _bk[0] += 1
            return b

        _tmp = [0]

        def tmp():
            t = TMP[:, _tmp[0] % 8, :]
            _tmp[0] += 1
            return t

        _tmpb = [0]

        def tmpb():
            t = TMPB[:, _tmpb[0] % 4, :]
            _tmpb[0] += 1
            return t

        _ws = [0]

        def wchunk(src_d, col0, dup=None):
            w = WS[:, _ws[0] % 4, :, :]
            _ws[0] += 1
            if dup is None:
                DMA("pool", w, src_d[:, col0:col0 + 128].rearrange("(c p) n -> p c n", p=128))
            else:
                DMA("pool", w[:, :, 0:64], src_d[:, col0:col0 + 64].rearrange("(c p) n -> p c n", p=128))
                DMA("pool", w[:, :, 64:128], src_d[:, col0:col0 + 64].rearrange("(c p) n -> p c n", p=128))
            return w

        def zmm(w, src, c0, n):
            b = bank()
            for k in range(8):
                MM(b[:, 0:n], w[:, k, :], src[:, k, c0:c0 + n], start=(k == 0), stop=(k == 7))
            return b

        DMA("sp", pvec[:], pvec_d[:, :])
        DMA("sp", cstf, cst_d[:, :, :])
        DMA("sp", csTf[:], csT_d[:, :, :])
        DMA("sp", kgbc[:], kgbc_d[:, :])
        DMA("sp", relf[:, 0:16], rel_d[:, :])
        DMA("sp", relf[:, 16:399], ohe_d[:, :])
        DMA("sp", scT[:], scT_d[:, :, :, :])
        DMA("sp", shT[:], shT_d[:, :, :])
        DMA("pool", wrab[:, 0, :, :], wra_d[:, :, :])
        DMA("pool", wrab[:, 1, :, :], wrx_d[:, :, :])
        CP(cstb[:], cstf)
        CP(identF[:], cstf[:, 0, :])
        CP(relb[:, 0:399], relf[:, 0:399])
        MSET(ones_b[:], 1.0)
        MSET(SSQ[:], 0.0)
        epsc = smalloc(1)
        MSET(epsc, EPS)
        onec = smalloc(1)
        MSET(onec, 1.0)
        zeroc = smalloc(1)
        MSET(zeroc, 0.0)
        sixt = smalloc(1)
        MSET(sixt, 1.0 / 16.0)
        CP(cT17[:, :, 0:1], pv("cp", 0, 8).rearrange("p (c o) -> p c o", o=1))
        CP(cT17[:, :, 1:17], csTf[:])
        ng32 = smalloc(8)
        TS(ng32, pv("ng", 0, 8), 32.0, None, ALU.mult)
        bsc1 = smalloc(8)
        TS(bsc1, pv("bsc", 0, 8), 1.0, None, ALU.add)
        hbra = smalloc(8)
        TS(hbra, pv("bra", 0, 8), 0.5, None, ALU.mult)
        hbrx = smalloc(8)
        TS(hbrx, pv("brx", 0, 8), 0.5, None, ALU.mult)
        esk = smalloc(8)
        ACT(esk, pv("snk", 0, 8), AF.Exp)
        spl = smalloc(8)
        ACT(spl, pv("lam", 0, 8), AF.Exp, scale=-1.0)
        ACT(spl, spl, AF.Ln, bias=onec)
        hc = smalloc(8)
        TS(hc, spl, -4.0, None, ALU.mult)
        cc = smalloc(8)
        TS(cc, spl, -8.0, None, ALU.mult)
        mhc2 = smalloc(8)
        TS(mhc2, spl, 2.0, None, ALU.mult)
        pm = pv("pm")
        pvf = pv("pv")

        P.mark("setup")
        mps = banks[5]
        for g6 in range(4):
            wb = WB[:, g6 % 2, :, :]
            DMA("pool", wb, wada_d[:, g6 * 512:(g6 + 1) * 512].rearrange("(c p) n -> p c n", p=128))
            for e4 in range(4):
                e = g6 * 4 + e4
                for k in range(8):
                    MM(mps[:, e * 17:(e + 1) * 17], wb[:, k, e4 * 128:(e4 + 1) * 128], cT17[:, k, :],
                       start=(k == 0), stop=(k == 7))
        P.mark("modmm")
        for c in range(8):
            TS(modS[:, c, :], mps[:, c * 17:(c + 1) * 17], pv("bsh", c), None, ALU.add)
            if c == 0:
                P.mark("mod0")
            TS(modS[:, 8 + c, :], mps[:, (8 + c) * 17:(9 + c) * 17], bsc1[:, c:c + 1], ng32[:, c:c + 1], ALU.add,
               ALU.mult)

        def build_gates():
            DMA("pool", rowb, bgate_d[:, :])
            cpbc = tmpb().rearrange("p (c m) -> p c m", c=4)
            for gi in range(2):
                wb = WB[:, gi, :, :]
                DMA("pool", wb, wada_d[:, 2048 + gi * 512:2048 + (gi + 1) * 512].rearrange("(c p) n -> p c n",
                                                                                             p=128))
                gp = banks[5]
                gs_ = banks[6]
                for k in range(8):
                    if k % 4 == 0:
                        for c4 in range(4):
                            CP(cpbc[:, c4, :], pv("cp", k + c4).to_broadcast([128, 128]))
                    MM(gp[:, :], cpbc[:, k % 4, :], wb[:, k, :], start=(k == 0), stop=False)
                MM(gp[:, :], ones_b[0:1, 0:128], rowb[:, gi * 512:(gi + 1) * 512], start=False, stop=True)
                for k in range(8):
                    MM(gs_[0:NS, :], cT17[:, k, 1:17], wb[:, k, :], start=(k == 0), stop=False)
                MM(gs_[0:NS, :], ones_b[0:1, 0:NS], rowb[:, gi * 512:(gi + 1) * 512], start=False, stop=True)
                ACT(GBC[:, gi * 512:(gi + 1) * 512], gp[:, :], AF.Copy, scale=0.5)
                ACT(GS[:, gi * 512:(gi + 1) * 512], gs_[0:NS, :], AF.Copy, scale=0.5)

        def build_diag():
            for c in range(8):
                for j in range(4):
                    TS(DG[:, c, j, :], ident, pv(f"wc{j}", c), None, ALU.mult)

        def phase_norm(x_d, ntile, rows, dstT, col0, smp):
            for i in range(ntile):
                xt = XT[0:rows, i % 2, :]
                DMA("sp", xt, x_d[i * 128:i * 128 + rows, :])
                ACT(XN[0:rows, i % 2, :], xt, AF.Square, accum=SSQ[0:rows, i:i + 1])
            n = ntile
            TS(SSQ[0:rows, 0:n], SSQ[0:rows, 0:n], 1024.0 * EPS, None, ALU.add)
            ACT(SSQ[0:rows, 20:20 + n], SSQ[0:rows, 0:n], AF.Ln)
            ACT(SSQ[0:rows, 20:20 + n], SSQ[0:rows, 20:20 + n], AF.Exp, scale=-0.5)
            for i in range(ntile):
                xt = XT[0:rows, i % 2, :]
                DMA("sp", xt, x_d[i * 128:i * 128 + rows, :])
                ACT(xt, xt, AF.Copy, scale=SSQ[0:rows, 20 + i:21 + i])
                for hlf in range(2):
                    pt = bank()
                    for c4 in range(4):
                        c = hlf * 4 + c4
                        idn = identF[0:rows, 0:rows]
                        o_ = pt[:, c4 * 128:c4 * 128 + rows]
                        i_ = xt[:, c * 128:(c + 1) * 128]
                        P.add("pe", (lambda o_=o_, i_=i_, idn=idn: (lambda e: e.transpose(out=o_, in_=i_, identity=idn)))(),
                              reads=[i_, idn], writes=[o_])
                    for c4 in range(4):
                        c = hlf * 4 + c4
                        src_ = pt[:, c4 * 128:c4 * 128 + rows]
                        if not smp:
                            TS(dstT[:, c, col0 + i * 128:col0 + i * 128 + rows], src_,
                               modS[:, 8 + c, 0:1], modS[:, c, 0:1], ALU.mult, ALU.add)
                        else:
                            t = tmp()
                            TT(t[:, 0:NS], src_, modS[:, 8 + c, 1:17], ALU.mult)
                            TT(dstT[:, c, col0:col0 + NS], t[:, 0:NS], modS[:, c, 1:17], ALU.add)

        def phase_norm_gen(x_d, ntile, rows, dstT, col0):
            for i in range(ntile):
                xt = XT[0:rows, i % 2, :]
                DMA("sp", xt, x_d[i * 128:i * 128 + rows, :])
                ACT(XN[0:rows, i % 2, :], xt, AF.Square, accum=SSQ[0:rows, i:i + 1])
                yield
            n = ntile
            TS(SSQ[0:rows, 0:n], SSQ[0:rows, 0:n], 1024.0 * EPS, None, ALU.add)
            ACT(SSQ[0:rows, 20:20 + n], SSQ[0:rows, 0:n], AF.Ln)
            ACT(SSQ[0:rows, 20:20 + n], SSQ[0:rows, 20:20 + n], AF.Exp, scale=-0.5)
            yield
            for i in range(ntile):
                xt = XT[0:rows, i % 2, :]
                DMA("sp", xt, x_d[i * 128:i * 128 + rows, :])
                ACT(xt, xt, AF.Copy, scale=SSQ[0:rows, 20 + i:21 + i])
                yield
                for hlf in range(2):
                    pt = banks[5 + hlf]
                    for c4 in range(4):
                        c = hlf * 4 + c4
                        idn = identF[0:rows, 0:rows]
                        o_ = pt[:, c4 * 128:c4 * 128 + rows]
                        i_ = xt[:, c * 128:(c + 1) * 128]
                        P.add("pe", (lambda o_=o_, i_=i_, idn=idn: (lambda e: e.transpose(out=o_, in_=i_, identity=idn)))(),
                              reads=[i_, idn], writes=[o_])
                    yield
                    for c4 in range(4):
                        c = hlf * 4 + c4
                        TS(dstT[:, c, col0 + i * 128:col0 + i * 128 + rows], pt[:, c4 * 128:c4 * 128 + rows],
                           modS[:, 8 + c, 0:1], modS[:, c, 0:1], ALU.mult, ALU.add)
                    yield

        hcars = [smalloc(1), smalloc(1)]
        RSs = sb("RSs", [128, 2, 2, NS], F32)

        def rnn_gen(c, src, h0ap, main, ch):
            RA, RI = (RARR_R if main else RARR_P)[ch]
            xrb = XRb[:, ch, :]
            hcar = hcars[ch]
            tcnt = [0]
            bcnt = [0]

            def T():
                t = TMP[:, ch * 4 + tcnt[0] % 4, :]
                tcnt[0] += 1
                return t

            def TB():
                t = TMPB[:, ch * 2 + bcnt[0] % 2, :]
                bcnt[0] += 1
                return t

            wx = wchunk(win_d, OFF_XR + c * 128)
            wg = wchunk(win_d, OFF_GR + c * 128) if main else None
            tiles = TILES if main else TILES[0:4]
            if main:
                TS(xrb[:, 1:4], XTL[:, c, :], pvf, None, ALU.mult)
            else:
                MSET(xrb[:, 1:4], 0.0)
            for (c0, n) in tiles:
                b = zmm(wx, src, c0, n)
                ACT(xrb[:, 4 + c0:4 + c0 + n], b[:, 0:n], AF.Copy)
                if main and n == NS:
                    CP(XRs[:, c, :], b[:, 0:NS])
                if main and c0 == 1536:
                    CP(CVP[:, c, :], b[:, 509:512])
                if (not main) and c0 == 1536:
                    CP(XTL[:, c, :], b[:, 509:512])
                yield
            hprev = h0ap
            batches = [tiles[0:2], tiles[2:]]
            for bt in batches:
                base = bt[0][0]
                for (c0, n) in bt:
                    smp = (n == NS)
                    lo = c0 - base
                    xc = T()
                    if not smp:
                        xcp = bank()
                        for j in range(4):
                            MM(xcp[:, 0:n], DG[:, c, j, :], xrb[:, c0 + 1 + j:c0 + 1 + j + n], start=(j == 0), stop=(j == 3))
                        yield
                        ACT(xc[:, 0:n], xcp[:, 0:n], AF.Identity, bias=pv("bcv", c))
                    else:
                        TS(xc[:, 0:n], XRs[:, c, :], pv("wc3", c), pv("bcv", c), ALU.mult, ALU.add)
                        for j in range(3):
                            STT(xc[:, 0:n], scT[:, c, j, :], pv(f"wc{j}", c), xc[:, 0:n], ALU.mult, ALU.add)
                        yield
                    yield
                    xcb = TB()
                    CP(xcb[:, 0:n], xc[:, 0:n])
                    yield
                    gi_ = bank()
                    MM(gi_[:, 0:n], wrab[:, 1, c, :], xcb[:, 0:n])
                    gr_ = bank()
                    MM(gr_[:, 0:n], wrab[:, 0, c, :], xcb[:, 0:n])
                    yield
                    t_i = T()
                    ACT(t_i[:, 0:n], gi_[:, 0:n], AF.Tanh, scale=0.5, bias=hbrx[:, c:c + 1])
                    ACT(RA[:, lo:lo + n], gr_[:, 0:n], AF.Tanh, scale=0.5, bias=hbra[:, c:c + 1])
                    yield
                    STT(RI[:, lo:lo + n], t_i[:, 0:n], 1.0, xc[:, 0:n], ALU.add, ALU.mult)
                    yield
                W01s = TMP[:, ch * 4:ch * 4 + 2, :].rearrange("p a n -> p (a n)")
                W23s = TMP[:, ch * 4 + 2:ch * 4 + 4, :].rearrange("p a n -> p (a n)")
                nb1 = sum(n for (_, n) in bt if n != NS)
                segs = [(0, nb1, W23s[:, 0:nb1], W01s[:, 0:nb1])]
                if any(n == NS for (_, n) in bt):
                    segs.append((nb1, NS, RSs[:, ch, 0, :], RSs[:, ch, 1, :]))
                for (o_, n_, wt, we) in segs:
                    ACT(wt, RA[:, o_:o_ + n_], AF.Tanh, scale=mhc2[:, c:c + 1], bias=mhc2[:, c:c + 1])
                    ACT(we, RA[:, o_:o_ + n_], AF.Exp, scale=hc[:, c:c + 1], bias=hc[:, c:c + 1])
                yield
                for (o_, n_, wt, we) in segs:
                    STT(wt, we, 1.0, wt, ALU.add, ALU.mult)
                yield
                for (o_, n_, wt, we) in segs:
                    TS(RA[:, o_:o_ + n_], wt, -1.0, 1.0, ALU.mult, ALU.add)
                yield
                W01 = TMP[:, ch * 4:ch * 4 + 2, :].rearrange("p a n -> p (a n)")
                W23 = TMP[:, ch * 4 + 2:ch * 4 + 4, :].rearrange("p a n -> p (a n)")
                nb = sum(n for (_, n) in bt if n != NS)
                has_s = any(n == NS for (_, n) in bt)
                ACT(W01[:, 0:nb], RA[:, 0:nb], AF.Square)
                if has_s:
                    ACT(W23[:, 0:NS], RA[:, nb:nb + NS], AF.Square)
                yield
                ACT(W01[:, 0:nb], W01[:, 0:nb], AF.Sqrt, scale=-1.0 / 16.0, bias=sixt)
                if has_s:
                    ACT(W23[:, 0:NS], W23[:, 0:NS], AF.Sqrt, scale=-1.0 / 16.0, bias=sixt)
                yield
                TT(RI[:, 0:nb], RI[:, 0:nb], W01[:, 0:nb], ALU.mult)
                if has_s:
                    TT(RI[:, nb:nb + NS], RI[:, nb:nb + NS], W23[:, 0:NS], ALU.mult)
                yield
                SCAN(W23[:, 0:nb], RA[:, 0:nb], RI[:, 0:nb], hprev)
                CP(hcar, W23[:, nb - 1:nb])
                hprev = hcar
                if base == 1024:
                    if main:
                        TS(HL[:, 8 + c:9 + c], W23[:, nb - 1:nb], 2.0, None, ALU.mult)
                    else:
                        TS(HL[:, c:c + 1], W23[:, nb - 1:nb], pvf, None, ALU.mult)
                yield
                if main:
                    k_ = 0
                    for (c0, n) in bt:
                        lo = c0 - base
                        if n == NS:
                            hs = W01[:, 0:NS]
                            TS(hs, shT[:, c, :], 0.5, None, ALU.mult)
                            TT(hs, hs, RA[:, lo:lo + n], ALU.mult)
                            TT(hs, hs, RI[:, lo:lo + n], ALU.add)
                            TS(HSs[:, c, :], hs, 2.0, None, ALU.mult)
                            tg = W01[:, 512:512 + NS]
                        else:
                            hs = W23[:, lo:lo + n]
                            tg = W01[:, k_ * 512:k_ * 512 + n]
                            k_ += 1
                        gb = zmm(wg, src, c0, n)
                        yield
                        ACT(tg, gb[:, 0:n], AF.Tanh, scale=0.5)
                        yield
                        STT(tg, tg, 1.0, gb[:, 0:n], ALU.add, ALU.mult)
                        TT(YR[:, c, c0:c0 + n], hs, tg, ALU.mult)
                        yield

        def run_pairs(gens, background=()):
            bg = list(background)
            for i in range(0, len(gens), 2):
                alive = list(gens[i:i + 2])
                while alive:
                    for g_ in list(alive):
                        try:
                            next(g_)
                        except StopIteration:
                            alive.remove(g_)
                    for g_ in list(bg):
                        try:
                            next(g_)
                        except StopIteration:
                            bg.remove(g_)
            for g_ in bg:
                for _ in g_:
                    pass

        _pool[:] = [0, 1, 2, 3, 4, 7]
        build_diag()
        phase_norm(xq_d, 16, 128, YA, 0, False)
        MSET(HL[:], 0.0)
        run_pairs([rnn_gen(c, YA, zeroc, False, c % 2) for c in range(8)],
                  background=[phase_norm_gen(xp_d, 16, 128, hT, 0)])

        P.mark("P")
        _pool[:] = [0, 1, 2, 3, 4, 5, 6, 7]
        eb = bank()
        extf = tmp()[0:16, 0:384]
        MM(eb[0:16, 0:383], relb[:, 0:16], relb[:, 16:399])
        ACT(extf[:, 0:383], eb[0:16, 0:383], AF.Copy, scale=8.0)
        P.mark("bias0")
        DMA("sp", ext_d[:, :], extf[:, 0:383])
        P.mark("bias1")
        for q8 in range(8):
            src = bass.AP(ext_d.tensor, q8 * 2 * 383, [[1, 128], [383, 2], [1, 256]])
            hk = tmp().rearrange("p (h q) -> p h q", h=2)
            DMA("sp", hk, src)
            P.mark("bias2")
            hkb = tmpb()
            CP(hkb[:, :], hk.rearrange("p h q -> p (h q)"))
            b = bank()
            P.mark("bias3")
            MM(b[:, :], Jm, hkb[:, :])
            hh = q8 * 2
            P.mark("bias4")
            CP(B8[:, hh:hh + 2, :].rearrange("p h q -> p (h q)"), b[:, :])
            P.mark("bias5")
            for h1 in range(2):
                TS(BS[:, hh + h1:hh + h1 + 1], b[:, h1 * 256 + 127:h1 * 256 + 128], 0.125, None, ALU.mult)

        P.mark("bias")
        phase_norm(xs_d, 1, NS, hT, NTOK, True)

        P.mark("MA")
        def qk_norm(b, n, gcol, dst):
            sq = tmpb()
            ACT(sq[:, 0:n], b[:, 0:n], AF.Square)
            sb_ = bank()
            MM(sb_[:, 0:n], bones, sq[:, 0:n])
            rl = tmp()
            ACT(rl[:, 0:n], sb_[:, 0:n], AF.Ln, bias=epsc)
            ACT(rl[:, 0:n], rl[:, 0:n], AF.Exp, scale=-0.5)
            STT(dst, b[:, 0:n], gcol, rl[:, 0:n], ALU.mult, ALU.mult)

        wkv = WB[:, 0, :, :]
        DMA("pool", wkv[:, :, 0:256], win_d[:, OFF_K:OFF_K + 256].rearrange("(c p) n -> p c n", p=128))
        DMA("pool", wkv[:, :, 256:512], win_d[:, OFF_V:OFF_V + 256].rearrange("(c p) n -> p c n", p=128))
        for blk in range(17):
            MSET(Vb[:, blk, :, 64:128], 1.0)

        def tokmajor(src, c0, rows, wcols):
            b = bank()
            for k in range(8):
                MM(b[0:rows, 0:256], src[:, k, c0:c0 + rows], wkv[:, k, wcols:wcols + 256], start=(k == 0),
                   stop=(k == 7))
            return b

        def emit_vblock(blk):
            if blk < 16:
                b = tokmajor(hT, blk * 128, 128, 256)
            else:
                b = tokmajor(YA, 1920, 128, 256)
            CP(Vb[:, blk, :, 0:64], b[:, 0:256].rearrange("p (g e) -> p g e", g=4))
            if blk == 15:
                CP(TK[:, 0, :], b[:, 0:256])
                DMA("sp", vwp_d[:, :], TK[:, 0, :], is_out=True)

        vi = 0
        for g in range(4):
            wk = wchunk(win_d, OFF_K + g * 64, dup=True)
            for (src_, c0, n, dst_) in [(hT, t0, tn, KT[:, g, t0:t0 + tn]) for (t0, tn) in TILES[0:4]] + \
                                       [(YA, 1920, 128, KT[:, g, 2064:2192])]:
                b = zmm(wk, src_, c0, n)
                if vi < 17:
                    emit_vblock(vi)
                    vi += 1
                qk_norm(b, n, pv("kg"), dst_)
        while vi < 17:
            emit_vblock(vi)
            vi += 1

        def tok_knorm(b, rows, dst):
            junk = TK[0:rows, 3, :]
            for g in range(4):
                ACT(junk[:, g * 64:(g + 1) * 64], b[0:rows, g * 64:(g + 1) * 64], AF.Square,
                    accum=SSQ[0:rows, 36 + g:37 + g])
            ACT(SSQ[0:rows, 36:40], SSQ[0:rows, 36:40], AF.Ln, scale=1.0 / 64.0, bias=epsc[0:rows, :])
            ACT(SSQ[0:rows, 36:40], SSQ[0:rows, 36:40], AF.Exp, scale=-0.5)
            for g in range(4):
                STT(dst[:, g * 64:(g + 1) * 64], b[0:rows, g * 64:(g + 1) * 64], SSQ[0:rows, 36 + g:37 + g],
                    kgbc[0:rows, :], ALU.mult, ALU.mult)

        b = tokmajor(hT, 1920, 128, 0)
        tok_knorm(b, 128, TK[:, 1, :])
        DMA("sp", kwp_d[:, :], TK[:, 1, :], is_out=True)
        b = tokmajor(hT, NTOK, NS, 0)
        tok_knorm(b, NS, TK[0:NS, 2, :])
        DMA("sp", knew_d[:, :], TK[0:NS, 2, :])
        DMA("sp", kws_d[:, 127, :], TK[0:NS, 2, :], is_out=True)
        b = tokmajor(hT, NTOK, NS, 256)
        CP(TK[0:NS, 0, :], b[0:NS, 0:256])
        DMA("sp", vnew_d[:, :], TK[0:NS, 0, :])
        DMA("sp", vws_d[:, 127, :], TK[0:NS, 0, :], is_out=True)
        DMA("sp", kws_d[:, 0:127, :], ck_d[:, 1:128, :], is_out=True)
        DMA("sp", vws_d[:, 0:127, :], cv_d[:, 1:128, :], is_out=True)
        DMA("sp", cvs12_d[:, :, :], sc12_d[:, :, :], is_out=True)

        P.mark("KV")
        _pool[:] = [0, 1, 2, 3]
        for j in range(8):
            g = j // 2
            qt = QT
            wga = wchunk(win_d, OFF_GA + j * 128)
            for (c0, n) in TILES:
                b = zmm(wga, hT, c0, n)
                e1 = tmp()
                ACT(e1[:, 0:n], b[:, 0:n], AF.Exp, scale=-1.0)
                ACT(e1[:, 0:n], e1[:, 0:n], AF.Ln, bias=onec)
                ACT(e1[:, 0:n], e1[:, 0:n], AF.Exp, scale=-1.0)
                if n == NS:
                    TT(SGS[:, j, :], e1[:, 0:n], b[:, 0:n], ALU.mult)
                else:
                    TT(SG[:, c0:c0 + n], e1[:, 0:n], b[:, 0:n], ALU.mult)
            wq = wchunk(win_d, OFF_Q + j * 128)
            for (c0, n) in TILES:
                b = zmm(wq, hT, c0, n)
                qk_norm(b, n, pv("qg"), qt[:, c0:c0 + n])
            CP(QTs[:, j, :], qt[:, NTOK:TW])
            def s_unit(m):
                kcol = 2064 if m < 0 else m * 128
                q0 = max(m, 0) * 128
                nq = 128 if (m < 0 or m == 15) else 256
                bo = 128 if m < 0 else 0
                sbk = bank()
                ebuf = EB[:, (m + 1) % 3, :, :]
                for hh in range(2):
                    rs_ = slice(hh * 64, hh * 64 + 64)
                    MM(sbk[:, hh * 256:hh * 256 + nq], KT[rs_, g, kcol:kcol + 128], qt[rs_, q0:q0 + nq],
                       start=True, stop=False)
                    MM(sbk[:, hh * 256:hh * 256 + nq], ident, B8[:, 2 * j + hh, bo:bo + nq], start=False, stop=True)
                for hh in range(2):
                    ACT(ebuf[:, hh, 0:nq], sbk[:, hh * 256:hh * 256 + nq], AF.Exp, scale=0.125,
                        bias=(pm if m < 0 else zeroc))

            def pv_unit(m):
                ebuf = EB[:, (m + 1) % 3, :, :]
                n_ = m
                ob = banks[4 + 2 * ((n_ // 4) % 2)]
                db = banks[5 + 2 * ((n_ // 4) % 2)]
                cs = slice((n_ % 4) * 128, (n_ % 4) * 128 + 128)
                eprev = EB[:, m % 3, :, :]
                pblk = 16 if n_ == 0 else n_ - 1
                pcol = slice(0, 128) if n_ == 0 else slice(128, 256)
                for hh in range(2):
                    os_ = slice(hh * 64, hh * 64 + 64)
                    tp = (0, 64 * hh)
                    MM(ob[os_, cs], Vb[:, pblk, g, 0:64], eprev[:, hh, pcol], start=True, stop=False, tp=tp)
                    MM(ob[os_, cs], Vb[:, n_, g, 0:64], ebuf[:, hh, 0:128], start=False, stop=True, tp=tp)
                    MM(db[os_, cs], Vb[:, pblk, g, 64:128], eprev[:, hh, pcol], start=True, stop=False, tp=tp)
                    MM(db[os_, cs], Vb[:, n_, g, 64:128], ebuf[:, hh, 0:128], start=False, stop=True, tp=tp)
                if n_ % 4 == 3:
                    tt_ = n_ // 4
                    ld = tmp()
                    ACT(ld[:, :], db[:, :], AF.Ln, bias=esk[:, j:j + 1])
                    ACT(ld[:, :], ld[:, :], AF.Exp, scale=-1.0)
                    TT(ld[:, :], ld[:, :], SG[:, tt_ * 512:(tt_ + 1) * 512], ALU.mult)
                    TT(YA[:, j, tt_ * 512:(tt_ + 1) * 512], ob[:, :], ld[:, :], ALU.mult)

            s_unit(-1)
            s_unit(0)
            for m in range(0, 16):
                if m + 1 <= 15:
                    s_unit(m + 1)
                pv_unit(m)

        Keff = KV[:, 0:4096].rearrange("p (b f) -> p b f", b=NS)
        Veff = KV[:, 4096:8192].rearrange("p (b f) -> p b f", b=NS)
        for bsm in range(NS):
            DMA("pool", Keff[0:127, bsm, :], ck_d[bsm, 1:128, :])
            DMA("pool", Veff[0:127, bsm, :], cv_d[bsm, 1:128, :])
            DMA("pool", Keff[127:128, bsm, :], knew_d[bsm:bsm + 1, :])
            DMA("pool", Veff[127:128, bsm, :], vnew_d[bsm:bsm + 1, :])
        P.mark("sattdma")
        ssb = banks[5]
        for bsm in range(NS):
            ktp = bank()
            for g in range(4):
                for hh in range(2):
                    MM(ktp[hh * 64:hh * 64 + 64, g * 128:(g + 1) * 128], Keff[:, bsm, g * 64:(g + 1) * 64], ident,
                       tp=(0, 64 * hh))
            kts = TMPB[:, bsm % 2, :]
            CP(kts, ktp[:, :])
            for g in range(4):
                for hh in range(2):
                    rs_ = slice(hh * 64, hh * 64 + 64)
                    c_ = bsm * 16 + 4 * g + hh
                    MM(ssb[:, c_:c_ + 3:2], kts[rs_, g * 128:(g + 1) * 128], QTs[rs_, 2 * g:2 * g + 2, bsm])
        P.mark("satts")
        ssv = ssb[:, 0:256].rearrange("p (b h) -> p b h", h=16)
        for h in range(16):
            ACT(ESs[:, :, h], ssv[:, :, h], AF.Exp, scale=0.125, bias=BS[:, h:h + 1])
        osb = banks[6]
        for bsm in range(NS):
            for g in range(4):
                for hh in range(2):
                    os_ = slice(hh * 64, hh * 64 + 64)
                    c_ = (2 * g) * 16 + bsm
                    MM(osb[os_, c_:c_ + 17:16], Veff[:, bsm, g * 64:(g + 1) * 64],
                       ESs[:, bsm, 4 * g + hh:4 * g + hh + 3:2], tp=(0, 64 * hh))
        dsb = bank()
        MM(dsb[:, 0:256], ones_b[:], ESs[:, :, :].rearrange("p b h -> p (b h)"))
        dsv = dsb[:, 0:256].rearrange("p (b h) -> p b h", h=16)
        for j in range(8):
            for hh in range(2):
                os_ = slice(hh * 64, hh * 64 + 64)
                CP(DSs[os_, j, :], dsv[os_, :, 2 * j + hh])
            ACT(DSs[:, j, :], DSs[:, j, :], AF.Ln, bias=esk[:, j:j + 1])
            ACT(DSs[:, j, :], DSs[:, j, :], AF.Exp, scale=-1.0)
            TT(DSs[:, j, :], DSs[:, j, :], SGS[:, j, :], ALU.mult)
            TT(YA[:, j, NTOK:TW], osb[:, j * 16:(j + 1) * 16], DSs[:, j, :], ALU.mult)

        P.mark("satt")
        _pool[:] = [0, 1, 2, 3, 4, 7]
        build_diag()
        run_pairs([rnn_gen(c, hT, HL[:, c:c + 1], True, c % 2) for c in range(8)])
        DMA("sp", hp_d[:, :], HL[:, 8:16], is_out=True)
        DMA("sp", cvp_d[:, :, :], CVP[:], is_out=True)
        DMA("sp", cvs_d[:, :, :], XRs[:], is_out=True)
        DMA("sp", hsm_d[:, :, :], HSs[:], is_out=True)

        P.mark("R")
        build_gates()
        for hlf in range(2):
            DMA("pool", WB[:, hlf, :, :], wout_d[:, hlf * 512:(hlf + 1) * 512].rearrange("(c p) n -> p c n", p=128))
        MGs = sb("MGs", [128, 8, NS], BF16)

        def g_elem(e, c0, n, wma, wmr, wpa, wpr, dst):
            bm = zmm(wma, hT, c0, n)
            tm = tmp()
            ACT(tm[:, 0:n], bm[:, 0:n], AF.Tanh, scale=0.5)
            br = zmm(wmr, hT, c0, n)
            tr_ = tmp()
            ACT(tr_[:, 0:n], br[:, 0:n], AF.Tanh, scale=0.5)
            bpa = zmm(wpa, YA, c0, n)
            STT(tm[:, 0:n], tm[:, 0:n], 1.0, bpa[:, 0:n], ALU.add, ALU.mult)
            bpr = zmm(wpr, YR, c0, n)
            STT(tr_[:, 0:n], tr_[:, 0:n], 1.0, bpr[:, 0:n], ALU.add, ALU.mult)
            TT(dst[:, e, 0:n], tm[:, 0:n], tr_[:, 0:n], ALU.add)

        def g_load(x_src, n, bi):
            rows = min(128, n - bi * 128)
            DMA("sp", XT[0:rows, bi % 2, :], x_src[bi * 128:bi * 128 + rows, :])

        def g_final(src, n, x_src, y_dst, gsel):
            nblk = (n + 127) // 128
            for bi in range(nblk):
                rows = min(128, n - bi * 128)
                xt = XT[0:rows, bi % 2, :]
                for hlf in range(2):
                    b = bank()
                    for e in range(8):
                        MM(b[0:rows, :], src[:, e, bi * 128:bi * 128 + rows], WB[:, hlf, e, :], start=(e == 0),
                           stop=(e == 7))
                    t = tmp()
                    gsrc = GS[:, hlf * 512:(hlf + 1) * 512] if gsel else GBC[:, hlf * 512:(hlf + 1) * 512]
                    TT(t[0:rows, :], b[0:rows, :], gsrc, ALU.mult)
                    TT(xt[:, hlf * 512:(hlf + 1) * 512], t[0:rows, :], xt[:, hlf * 512:(hlf + 1) * 512], ALU.add)
                DMA("sp", y_dst[bi * 128:bi * 128 + rows, :], xt, is_out=True)
                if bi + 2 < nblk:
                    g_load(x_src, n, bi + 2)

        WS2 = LOCb[:, 4096:8192].rearrange("p (s c n) -> p s c n", s=4, c=8)

        def gchunk(src_d, col0, pool_, slot):
            w = pool_[:, slot, :, :]
            DMA("pool", w, src_d[:, col0:col0 + 128].rearrange("(c p) n -> p c n", p=128))
            return w

        for ti, (c0, n) in enumerate(TILES[0:4]):
            g_load(xp_d[c0:c0 + n, :], n, 0)
            g_load(xp_d[c0:c0 + n, :], n, 1)
            for e in range(8):
                pool_ = WS if e % 2 == 0 else WS2
                wma = gchunk(win_d, OFF_MA + e * 128, pool_, 0)
                wmr = gchunk(win_d, OFF_MR + e * 128, pool_, 1)
                wpa = gchunk(wpa_d, e * 128, pool_, 2)
                wpr = gchunk(wpr_d, e * 128, pool_, 3)
                g_elem(e, c0, n, wma, wmr, wpa, wpr, MGt)
                if ti == 3:
                    g_elem(e, NTOK, NS, wma, wmr, wpa, wpr, MGs)
            g_final(MGt, n, xp_d[c0:c0 + n, :], y_d[c0:c0 + n, :], False)
        g_load(xs_d, NS, 0)
        g_final(MGs, NS, xs_d, ys_d, True)

        with ExitStack() as es2:
            sems = {e: es2.enter_context(nc.semaphore(f"ksem_{e}")) for e in ("pe", "act", "dve", "pool")}
            dma_sems = {"sp": [es2.enter_context(nc.semaphore(f"d_sp{i}")) for i in range(24)],
                        "pool": [es2.enter_context(nc.semaphore(f"d_pl{i}")) for i in range(12)]}
            run = P.emit_all(sems, dma_sems)
            with nc.Block() as block:
                @block.tensor
                def _(e):
                    run("pe", e)

                @block.scalar
                def _(e):
                    run("act", e)

                @block.vector
                def _(e):
                    run("dve", e)

                @block.gpsimd
                def _(e):
                    run("pool", e)

                @block.sync
                def _(e):
                    run("sp", e)
    return nc


def _fm(v):
    return np.ascontiguousarray(np.asarray(v, np.float32).reshape(8, 128).T)


_NC_CACHE = {}


def kernel(x_prompt, x_sample, cache_k, cache_v, state_conv, state_h, c_prompt, c_sample, rel_table, norm_g,
           w_ada, b_ada, w_in, q_norm_g, k_norm_g, sinks, w_conv, b_conv, w_rg_a, b_rg_a, w_rg_x, b_rg_x,
           lru_lambda, w_proj_attn, w_proj_rnn, w_out):
    f32 = np.float32
    x_prompt = np.asarray(x_prompt, f32)
    x_sample = np.asarray(x_sample, f32)
    ident = np.eye(128, dtype=f32)
    Jm = ident[::-1].copy()
    bones = np.zeros((128, 128), f32)
    bones[0:64, 0:64] = 1.0 / 64.0
    bones[64:128, 64:128] = 1.0 / 64.0
    cst = np.ascontiguousarray(np.stack([ident, Jm, bones], axis=1))
    import math
    ohe = np.zeros((33, 383), f32)
    for dd in range(128):
        nn = max(dd, 0)
        if nn < 16:
            bkt = nn
        else:
            nf = np.float32(max(nn, 1))
            bkt = 16 + int(np.int32(np.log(nf / np.float32(16)) / np.float32(math.log(128 / 16)) * np.float32(16)))
            bkt = min(bkt, 31)
        ohe[bkt, 127 + dd] = 1.0
    ohe[32, :] = NEGB
    ohe[32, 127:255] = 0.0
    relx = np.concatenate([np.asarray(rel_table, f32), np.ones((1, 16), f32)], axis=0)

    def blockdiag(w):
        w = np.asarray(w, f32)[0]
        o = np.zeros((128, 8, 128), f32)
        for c in range(8):
            for nl in range(2):
                o[nl * 64:(nl + 1) * 64, c, nl * 64:(nl + 1) * 64] = w[2 * c + nl]
        return o

    wra = blockdiag(w_rg_a)
    wrx = blockdiag(w_rg_x)
    b_ada0 = np.asarray(b_ada, f32)[0]
    kgbc = np.ascontiguousarray(np.broadcast_to(np.asarray(k_norm_g, f32)[0][None, :], (128, 64)))
    common = dict(w_ada=np.asarray(w_ada, f32)[0], w_in=np.asarray(w_in, f32)[0],
                  w_pa=np.asarray(w_proj_attn, f32)[0], w_pr=np.asarray(w_proj_rnn, f32)[0],
                  w_out=np.asarray(w_out, f32)[0], wra=wra, wrx=wrx, cst=cst, ohe=ohe, relx=relx,
                  bgate=np.ascontiguousarray(b_ada0[2048:3072][None, :]), kgbc=kgbc)
    in_maps = []
    for r in range(8):
        s, hf = r // 2, r % 2
        pvec = np.zeros((128, NPV), f32)

        def put(name, arr):
            o, w = PV[name]
            pvec[:, o:o + w] = arr.reshape(128, w)

        put("ng", _fm(np.asarray(norm_g, f32)[0]))
        put("bsh", _fm(b_ada0[0:1024]))
        put("bsc", _fm(b_ada0[1024:2048]))
        for j in range(4):
            put(f"wc{j}", _fm(np.asarray(w_conv, f32)[0, j]))
        put("bcv", _fm(np.asarray(b_conv, f32)[0]))
        put("bra", _fm(np.asarray(b_rg_a, f32)[0]))
        put("brx", _fm(np.asarray(b_rg_x, f32)[0]))
        put("lam", _fm(np.asarray(lru_lambda, f32)[0]))
        put("qg", np.tile(np.asarray(q_norm_g, f32)[0], 2)[:, None])
        put("kg", np.tile(np.asarray(k_norm_g, f32)[0], 2)[:, None])
        put("snk", np.repeat(np.asarray(sinks, f32)[0].reshape(8, 2), 64, axis=1).T)
        put("cp", _fm(np.asarray(c_prompt, f32)[s]))
        put("pv", np.full((128, 1), 1.0 if hf else 0.0, f32))
        put("pm", np.full((128, 1), 0.0 if hf else -1e30, f32))
        bs = slice(r * NS, (r + 1) * NS)
        cs = np.asarray(c_sample, f32)[bs]
        csT = np.ascontiguousarray(cs.T.reshape(8, 128, NS).transpose(1, 0, 2))
        sc = np.asarray(state_conv, f32)[0, bs]
        scT = np.ascontiguousarray(sc.transpose(2, 1, 0).reshape(8, 128, 3, NS).transpose(1, 0, 2, 3))
        sh = np.asarray(state_h, f32)[0, bs]
        shT = np.ascontiguousarray(sh.T.reshape(8, 128, NS).transpose(1, 0, 2))
        m = dict(common)
        m.update(xp=np.ascontiguousarray(x_prompt[s, hf * NTOK:(hf + 1) * NTOK]),
                 xq=(np.ascontiguousarray(x_prompt[s, 0:NTOK]) if hf else np.zeros((NTOK, D), f32)),
                 xs=np.ascontiguousarray(x_sample[bs, 0]),
                 ck=np.ascontiguousarray(np.asarray(cache_k, f32)[0, bs].reshape(NS, 128, 256)),
                 cv=np.ascontiguousarray(np.asarray(cache_v, f32)[0, bs].reshape(NS, 128, 256)),
                 scT=scT, shT=shT, sc12=np.ascontiguousarray(sc[:, 1:3, :]), pvec=pvec, csT=csT)
        in_maps.append(m)
    if "nc" not in _NC_CACHE:
        _NC_CACHE["nc"] = build_nc()
    nc = _NC_CACHE["nc"]
    res = run_bass_kernel_spmd(nc, in_maps, core_ids=list(range(8)))
    R = res.results

    y_p = np.zeros((4, 4096, D), f32)
    y_s = np.zeros((128, 1, D), f32)
    kwp = np.zeros((1, 4, 128, 4, 64), f32)
    vwp = np.zeros((1, 4, 128, 4, 64), f32)
    cvp = np.zeros((1, 4, 3, D), f32)
    hp = np.zeros((1, 4, D), f32)
    kws = np.zeros((1, 128, 128, 4, 64), f32)
    vws = np.zeros((1, 128, 128, 4, 64), f32)
    cvs = np.zeros((1, 128, 3, D), f32)
    hsm = np.zeros((1, 128, D), f32)
    for r in range(8):
        s, hf = r // 2, r % 2
        o = R[r]
        y_p[s, hf * NTOK:(hf + 1) * NTOK] = o["y"]
        bs = slice(r * NS, (r + 1) * NS)
        y_s[bs, 0] = o["ys"]
        if hf:
            kwp[0, s] = o["kwp"].reshape(128, 4, 64)
            vwp[0, s] = o["vwp"].reshape(128, 4, 64)
            cvp[0, s] = o["cvp"].transpose(2, 1, 0).reshape(3, D)
            hp[0, s] = o["hp"].T.reshape(D)
        kws[0, bs] = o["kws"].reshape(NS, 128, 4, 64)
        vws[0, bs] = o["vws"].reshape(NS, 128, 4, 64)
        cvs[0, bs, 0:2] = o["cvs12"]
        cvs[0, bs, 2] = o["cvs"].transpose(2, 1, 0).reshape(NS, D)
        hsm[0, bs] = o["hsm"].transpose(2, 1, 0).reshape(NS, D)
    return (y_p, y_s, kwp, vwp, cvp, hp, kws, vws, cvs, hsm)
```

```python
import numpy as np
import concourse.bass as bass
import concourse.mybir as mybir
from concourse.bass_utils import run_bass_kernel_spmd

F32 = mybir.dt.float32
BF16 = mybir.dt.bfloat16
AF = mybir.ActivationFunctionType
ALU = mybir.AluOpType
DSZ = {F32: 4, BF16: 2}

D = 1024
NTOK = 2048
NS = 16
TW = NTOK + NS
INW = 6656
EPS = 1e-6
OFF_Q, OFF_K, OFF_V, OFF_GA, OFF_XR, OFF_GR, OFF_MA, OFF_MR = 0, 1024, 1280, 1536, 2560, 3584, 4608, 5632
TILES = [(0, 512), (512, 512), (1024, 512), (1536, 512), (2048, 16)]
NEGB = -3750.0

PV = {}
_c = 0
for _n, _w in [("ng", 8), ("bsh", 8), ("bsc", 8), ("wc0", 8), ("wc1", 8), ("wc2", 8), ("wc3", 8), ("bcv", 8),
               ("bra", 8), ("brx", 8), ("lam", 8), ("qg", 1), ("kg", 1), ("snk", 8), ("cp", 8), ("pv", 1),
               ("pm", 1)]:
    PV[_n] = (_c, _w)
    _c += _w
NPV = _c


class Prog:
    def __init__(self, nc):
        self.nc = nc
        self.ops = []
        self.track = {}
        self.stopped = False
        self.kstop = ""

    def mark(self, name):
        if self.kstop == name:
            self.stopped = True

    @staticmethod
    def region(ap):
        t = ap.tensor
        name = t.name
        esz = DSZ.get(ap.dtype, 4)
        dims = list(ap.ap)
        off = ap.offset
        if type(t).__name__ not in ("SBTensorHandle", "PSumTensorHandle"):
            hi = off + sum((c - 1) * abs(st) for st, c in dims) + 1
            return (name, 0, 1, off * esz, hi * esz)
        if type(t).__name__ == "PSumTensorHandle":
            return (name, 0, 128, 0, 1 << 20)
        psz = 1
        for d_ in list(t.shape)[1:]:
            psz *= d_
        p0 = off // psz
        f0 = off % psz
        pcnt = dims[0][1]
        fhi = f0 + sum((c - 1) * abs(st) for st, c in dims[1:]) + 1
        return (name, p0, p0 + pcnt, f0 * esz, fhi * esz)

    @staticmethod
    def overlap(a, b):
        return a[1] < b[2] and b[1] < a[2] and a[3] < b[4] and b[3] < a[4]

    @staticmethod
    def contains(a, b):
        return a[1] <= b[1] and b[2] <= a[2] and a[3] <= b[3] and b[4] <= a[4]

    def add(self, eng, emit, reads=(), writes=(), dma=False, out=False):
        if self.stopped:
            return -1
        oid = len(self.ops)
        deps = set()
        rregs = [self.region(a) for a in reads]
        wregs = [self.region(a) for a in writes]
        wregs = wregs + [r for r in rregs if r[4] == (1 << 20)]
        rregs = [r for r in rregs if r[4] != (1 << 20)]
        for r in rregs:
            tr = self.track.setdefault(r[0], {"w": [], "r": []})
            for (reg, o) in tr["w"]:
                if self.overlap(reg, r):
                    deps.add(o)
        for w in wregs:
            tr = self.track.setdefault(w[0], {"w": [], "r": []})
            for (reg, o) in tr["w"]:
                if self.overlap(reg, w):
                    deps.add(o)
            for (reg, o) in tr["r"]:
                if self.overlap(reg, w):
                    deps.add(o)
        for r in rregs:
            self.track[r[0]]["r"].append((r, oid))
        for w in wregs:
            tr = self.track[w[0]]
            tr["w"] = [(reg, o) for (reg, o) in tr["w"] if not self.contains(w, reg)]
            tr["r"] = [(reg, o) for (reg, o) in tr["r"] if not self.contains(w, reg) or o == oid]
            tr["w"].append((w, oid))
        deps.discard(oid)
        self.ops.append(dict(eng=eng, emit=emit, deps=deps, dma=dma, out=out, tok=None, prev=None))
        return oid

    def emit_all(self, sems, dma_sems):
        ops = self.ops
        needed = set()
        for op in ops:
            needed |= op["deps"]
        cnt = {e: 0 for e in sems}
        dcnt = {}
        dk = {e: 0 for e in dma_sems}
        for i, op in enumerate(ops):
            e = op["eng"]
            if op["dma"]:
                pool = dma_sems[e]
                s = pool[dk[e] % len(pool)]
                dk[e] += 1
                prev = dcnt.get(s, 0)
                dcnt[s] = prev + 16
                op["tok"] = (s, prev + 16)
                op["prev"] = (s, prev)
            elif i in needed:
                cnt[e] += 1
                op["tok"] = (sems[e], cnt[e])
        by_eng = {}
        for i, op in enumerate(ops):
            by_eng.setdefault(op["eng"], []).append(i)

        def run(ename, eh):
            seen = {}
            for i in by_eng.get(ename, []):
                op = ops[i]
                waits = {}
                for d in op["deps"]:
                    dop = ops[d]
                    if ename == "pe" and dop["eng"] == "pe" and not dop["dma"]:
                        continue
                    s, v = dop["tok"]
                    waits[s] = max(waits.get(s, 0), v)
                if op["dma"] and op["prev"][1] > 0:
                    s, v = op["prev"]
                    waits[s] = max(waits.get(s, 0), v)
                for s, v in waits.items():
                    if seen.get(s, 0) < v:
                        eh.wait_ge(s, v)
                        seen[s] = v
                ins = op["emit"](eh)
                if op["tok"] is not None:
                    ins.then_inc(op["tok"][0], 16 if op["dma"] else 1)
            if ename == "sp":
                for s, v in dcnt.items():
                    if seen.get(s, 0) < v:
                        eh.wait_ge(s, v)
        return run


def build_nc():
    nc = bass.Bass("TRN2", target_bir_lowering=False)
    P = Prog(nc)

    def din(name, shape, dt=F32):
        return nc.dram_tensor(name, list(shape), dt, kind="ExternalInput").ap()

    def dout(name, shape, dt=F32):
        return nc.dram_tensor(name, list(shape), dt, kind="ExternalOutput").ap()

    xp_d = din("xp", [NTOK, D])
    xq_d = din("xq", [NTOK, D])
    xs_d = din("xs", [NS, D])
    ck_d = din("ck", [NS, 128, 256])
    cv_d = din("cv", [NS, 128, 256])
    scT_d = din("scT", [128, 8, 3, NS])
    shT_d = din("shT", [128, 8, NS])
    sc12_d = din("sc12", [NS, 2, D])
    pvec_d = din("pvec", [128, NPV])
    csT_d = din("csT", [128, 8, NS])
    bgate_d = din("bgate", [1, D])
    kgbc_d = din("kgbc", [128, 64])
    rel_d = din("relx", [33, 16])
    ohe_d = din("ohe", [33, 383])
    cst_d = din("cst", [128, 3, 128])
    wada_d = din("w_ada", [D, 3 * D])
    win_d = din("w_in", [D, INW])
    wra_d = din("wra", [128, 8, 128])
    wrx_d = din("wrx", [128, 8, 128])
    wpa_d = din("w_pa", [D, D])
    wpr_d = din("w_pr", [D, D])
    wout_d = din("w_out", [D, D])

    y_d = dout("y", [NTOK, D])
    ys_d = dout("ys", [NS, D])
    kwp_d = dout("kwp", [128, 256])
    vwp_d = dout("vwp", [128, 256])
    cvp_d = dout("cvp", [128, 8, 3])
    hp_d = dout("hp", [128, 8])
    kws_d = dout("kws", [NS, 128, 256])
    vws_d = dout("vws", [NS, 128, 256])
    cvs_d = dout("cvs", [128, 8, NS])
    cvs12_d = dout("cvs12", [NS, 2, D])
    hsm_d = dout("hsm", [128, 8, NS])
    ext_d = nc.dram_tensor("ext_scr", [16, 383], F32, kind="Internal").ap()
    knew_d = nc.dram_tensor("knew_scr", [NS, 256], F32, kind="Internal").ap()
    vnew_d = nc.dram_tensor("vnew_scr", [NS, 256], F32, kind="Internal").ap()

    from contextlib import ExitStack
    es = ExitStack()

    def sb(name, shape, dt=F32):
        return es.enter_context(nc.sbuf_tensor("sb_" + name, list(shape), dt))

    def ps(name, shape, dt=F32):
        return es.enter_context(nc.psum_tensor("ps_" + name, list(shape), dt))

    with es:
        hT = sb("hT", [128, 8, TW], BF16)
        YA = sb("YA", [128, 8, TW], BF16)
        KV = sb("KV", [128, 17920], BF16)
        KT = KV[:, 0:4 * 2192].rearrange("p (c n) -> p c n", c=4)
        Vb = KV[:, 8768:8768 + 17 * 512].rearrange("p (b g e) -> p b g e", b=17, g=4)
        YR = KV[:, 0:8 * TW].rearrange("p (c n) -> p c n", c=8)
        WB = sb("WB", [128, 2, 8, 512], BF16)
        WS = sb("WS", [128, 4, 8, 128], BF16)
        XT = sb("XT", [128, 2, D], F32)
        XN = sb("XN", [128, 2, D], BF16)
        TMP = sb("TMP", [128, 8, 512], F32)
        TMPB = sb("TMPB", [128, 4, 512], BF16)
        LOCb = sb("LOCb", [128, 8320], BF16)
        DG = LOCb[:, 0:4096].rearrange("p (c j m) -> p c j m", c=8, j=4)
        XRb = LOCb[:, 4096:4096 + 4160].rearrange("p (b n) -> p b n", b=2)
        B8 = LOCb[:, 0:4096].rearrange("p (h q) -> p h q", h=16)
        MGt = LOCb[:, 0:4096].rearrange("p (e n) -> p e n", e=8)
        GATES = sb("GATES", [128, 2080], F32)
        GBC = GATES[:, 0:1024]
        GS = GATES[0:NS, 1024:2048]
        SG = GATES[:, 0:TW]
        TK = GATES[:, 0:1024].rearrange("p (a f) -> p a f", a=4)
        ROt = sb("ROt", [128, 1040], F32)
        QTF = sb("QTF", [128, 1040], F32)
        QT = QTF[:, :].bitcast(BF16)[:, 0:TW]
        KVf = KV[:, :].bitcast(F32)
        RARR_P = [(KVf[:, 0:1040], KVf[:, 1040:2080]), (KVf[:, 2080:3120], KVf[:, 3120:4160])]
        RARR_R = [(GATES[:, 0:1040], GATES[:, 1040:2080]), (ROt[:, :], QTF[:, :])]
        EB = sb("EB", [128, 3, 2, 256], BF16)
        pvec = sb("pvec", [128, NPV], F32)
        cstf = XT[:, 1, 0:384].rearrange("p (a m) -> p a m", a=3)
        cstb = sb("cstb", [128, 3, 128], BF16)
        identF = sb("identF", [128, 128], F32)
        ones_b = sb("ones_b", [128, 128], BF16)
        small = sb("small", [128, 128], F32)
        cT17 = sb("cT17", [128, 8, 17], BF16)
        csTf = sb("csTf", [128, 8, NS], F32)
        modS = sb("modS", [128, 16, 17], F32)
        wrab = sb("wrab", [128, 2, 8, 128], BF16)
        kgbc = sb("kgbc", [128, 64], F32)
        rowb = XN[0:1, 1, :]
        relf = XT[0:33, 0, 0:400]
        relbt = sb("relb", [33, 400], BF16)
        relb = relbt[:, :]
        BS = sb("BS", [128, 16], F32)
        scT = sb("scT", [128, 8, 3, NS], F32)
        shT = sb("shT", [128, 8, NS], F32)
        HSs = sb("HSs", [128, 8, NS], F32)
        XRs = sb("XRs", [128, 8, NS], F32)
        HL = sb("HL", [128, 16], F32)
        XTL = sb("XTL", [128, 8, 3], F32)
        CVP = sb("CVP", [128, 8, 3], F32)
        SGS = sb("SGS", [128, 8, NS], F32)
        KTs = sb("KTs", [128, 2, 128], BF16)
        ESs = sb("ESs", [128, NS, 16], BF16)
        DSs = sb("DSs", [128, 8, NS], F32)
        SSQ = sb("SSQ", [128, 40], F32)
        QTs = sb("QTs", [128, 8, NS], BF16)

        banks = [ps(f"bk{i}", [128, 512], F32) for i in range(8)]

        ident = cstb[:, 0, :]
        Jm = cstb[:, 1, :]
        bones = cstb[:, 2, :]

        def pv(name, c=0, w=1):
            o, _ = PV[name]
            return pvec[:, o + c:o + c + w]

        _sm = [0]

        def smalloc(w):
            o = _sm[0]
            _sm[0] += w
            assert _sm[0] <= 128
            return small[:, o:o + w]

        def DMA(eng, out, in_, is_out=False):
            P.add(eng, lambda e: e.dma_start(out=out, in_=in_), reads=[in_], writes=[out], dma=True, out=is_out)

        def MM(out, lhsT, rhs, start=True, stop=True, tp=None):
            if tp is None:
                P.add("pe", lambda e: e.matmul(out, lhsT=lhsT, rhs=rhs, start=start, stop=stop),
                      reads=[lhsT, rhs], writes=[out])
            else:
                P.add("pe", lambda e: e.matmul(out, lhsT=lhsT, rhs=rhs, start=start, stop=stop, tile_position=tp),
                      reads=[lhsT, rhs], writes=[out])

        def TR(out, in_, rows=128):
            idn = ident[0:rows, 0:rows]
            P.add("pe", lambda e: e.transpose(out=out, in_=in_, identity=idn), reads=[in_, idn], writes=[out])

        def ACT(out, in_, func, scale=1.0, bias=None, accum=None):
            rd = [in_]
            kw = {}
            if isinstance(scale, float) or isinstance(scale, int):
                kw["scale"] = float(scale)
            else:
                kw["scale"] = scale
                rd.append(scale)
            if bias is not None:
                kw["bias"] = bias
                if not isinstance(bias, float):
                    rd.append(bias)
            wr = [out]
            if accum is not None:
                kw["accum_out"] = accum
                wr.append(accum)
            P.add("act", lambda e: e.activation(out=out, in_=in_, func=func, **kw), reads=rd, writes=wr)

        def TS(out, in0, s1, s2, op0, op1=None, eng="dve"):
            if op1 is None:
                rd = [in0] + ([] if isinstance(s1, (float, int)) else [s1])
                P.add(eng, lambda e: e.tensor_scalar(out=out, in0=in0, scalar1=s1, scalar2=None, op0=op0),
                      reads=rd, writes=[out])
                return
            assert isinstance(s1, (float, int)) == isinstance(s2, (float, int)), "mixed AP/imm scalars"
            rd = [in0] + [s_ for s_ in (s1, s2) if not isinstance(s_, (float, int))]
            P.add(eng, lambda e: e.tensor_scalar(out=out, in0=in0, scalar1=s1, scalar2=s2, op0=op0, op1=op1),
                  reads=rd, writes=[out])

        def TT(out, in0, in1, op, eng="dve"):
            P.add(eng, lambda e: e.tensor_tensor(out=out, in0=in0, in1=in1, op=op), reads=[in0, in1], writes=[out])

        def STT(out, in0, scalar, in1, op0, op1, eng="dve"):
            rd = [in0, in1] + ([] if isinstance(scalar, (float, int)) else [scalar])
            P.add(eng, lambda e: e.scalar_tensor_tensor(out=out, in0=in0, scalar=scalar, in1=in1, op0=op0, op1=op1),
                  reads=rd, writes=[out])

        def CP(out, in_, eng="dve"):
            P.add(eng, lambda e: e.tensor_copy(out=out, in_=in_), reads=[in_], writes=[out])

        def MSET(ap, val, eng="dve"):
            P.add(eng, lambda e: e.memset(ap, val), reads=[], writes=[ap])

        def RECIP(out, in_):
            P.add("dve", lambda e: e.reciprocal(out=out, in_=in_), reads=[in_], writes=[out])

        def SCAN(out, d0, d1, init):
            rd = [d0, d1] + ([] if isinstance(init, (float, int)) else [init])
            P.add("dve", lambda e: e.tensor_tensor_scan(out=out, data0=d0, data1=d1, initial=init,
                                                         op0=ALU.mult, op1=ALU.add), reads=rd, writes=[out])

        _bk = [0]
        _pool = [0, 1, 2, 3]

        def bank():
            b = banks[_pool[_bk[0] % len(_pool)]]
            _bk[0] += 1
            return b

        _tmp = [0]

        def tmp():
            t = TMP[:, _tmp[0] % 8, :]
            _tmp[0] += 1
            return t

        _tmpb = [0]

        def tmpb():
            t = TMPB[:, _tmpb[0] % 4, :]
            _tmpb[0] += 1
            return t

        _ws = [0]

        def wchunk(src_d, col0, dup=None):
            w = WS[:, _ws[0] % 4, :, :]
            _ws[0] += 1
            if dup is None:
                DMA("pool", w, src_d[:, col0:col0 + 128].rearrange("(c p) n -> p c n", p=128))
            else:
                DMA("pool", w[:, :, 0:64], src_d[:, col0:col0 + 64].rearrange("(c p) n -> p c n", p=128))
                DMA("pool", w[:, :, 64:128], src_d[:, col0:col0 + 64].rearrange("(c p) n -> p c n", p=128))
            return w

        def zmm(w, src, c0, n):
            b = bank()
            for k in range(8):
                MM(b[:, 0:n], w[:, k, :], src[:, k, c0:c0 + n], start=(k == 0), stop=(k == 7))
            return b

        DMA("sp", pvec[:], pvec_d[:, :])
        DMA("sp", cstf, cst_d[:, :, :])
        DMA("sp", csTf[:], csT_d[:, :, :])
        DMA("sp", kgbc[:], kgbc_d[:, :])
        DMA("sp", relf[:, 0:16], rel_d[:, :])
        DMA("sp", relf[:, 16:399], ohe_d[:, :])
        DMA("sp", scT[:], scT_d[:, :, :, :])
        DMA("sp", shT[:], shT_d[:, :, :])
        DMA("pool", wrab[:, 0, :, :], wra_d[:, :, :])
        DMA("pool", wrab[:, 1, :, :], wrx_d[:, :, :])
        CP(cstb[:], cstf)
        CP(identF[:], cstf[:, 0, :])
        CP(relb[:, 0:399], relf[:, 0:399])
        MSET(ones_b[:], 1.0)
        MSET(SSQ[:], 0.0)
        epsc = smalloc(1)
        MSET(epsc, EPS)
        onec = smalloc(1)
        MSET(onec, 1.0)
        zeroc = smalloc(1)
        MSET(zeroc, 0.0)
        sixt = smalloc(1)
        MSET(sixt, 1.0 / 16.0)
        CP(cT17[:, :, 0:1], pv("cp", 0, 8).rearrange("p (c o) -> p c o", o=1))
        CP(cT17[:, :, 1:17], csTf[:])
        ng32 = smalloc(8)
        TS(ng32, pv("ng", 0, 8), 32.0, None, ALU.mult)
        bsc1 = smalloc(8)
        TS(bsc1, pv("bsc", 0, 8), 1.0, None, ALU.add)
        hbra = smalloc(8)
        TS(hbra, pv("bra", 0, 8), 0.5, None, ALU.mult)
        hbrx = smalloc(8)
        TS(hbrx, pv("brx", 0, 8), 0.5, None, ALU.mult)
        esk = smalloc(8)
        ACT(esk, pv("snk", 0, 8), AF.Exp)
        spl = smalloc(8)
        ACT(spl, pv("lam", 0, 8), AF.Exp, scale=-1.0)
        ACT(spl, spl, AF.Ln, bias=onec)
        hc = smalloc(8)
        TS(hc, spl, -4.0, None, ALU.mult)
        cc = smalloc(8)
        TS(cc, spl, -8.0, None, ALU.mult)
        mhc2 = smalloc(8)
        TS(mhc2, spl, 2.0, None, ALU.mult)
        pm = pv("pm")
        pvf = pv("pv")

        P.mark("setup")
        mps = banks[5]
        for g6 in range(4):
            wb = WB[:, g6 % 2, :, :]
            DMA("pool", wb, wada_d[:, g6 * 512:(g6 + 1) * 512].rearrange("(c p) n -> p c n", p=128))
            for e4 in range(4):
                e = g6 * 4 + e4
                for k in range(8):
                    MM(mps[:, e * 17:(e + 1) * 17], wb[:, k, e4 * 128:(e4 + 1) * 128], cT17[:, k, :],
                       start=(k == 0), stop=(k == 7))
        P.mark("modmm")
        for c in range(8):
            TS(modS[:, c, :], mps[:, c * 17:(c + 1) * 17], pv("bsh", c), None, ALU.add)
            if c == 0:
                P.mark("mod0")
            TS(modS[:, 8 + c, :], mps[:, (8 + c) * 17:(9 + c) * 17], bsc1[:, c:c + 1], ng32[:, c:c + 1], ALU.add,
               ALU.mult)

        def build_gates():
            DMA("pool", rowb, bgate_d[:, :])
            cpbc = tmpb().rearrange("p (c m) -> p c m", c=4)
            for gi in range(2):
                wb = WB[:, gi, :, :]
                DMA("pool", wb, wada_d[:, 2048 + gi * 512:2048 + (gi + 1) * 512].rearrange("(c p) n -> p c n",
                                                                                             p=128))
                gp = banks[5]
                gs_ = banks[6]
                for k in range(8):
                    if k % 4 == 0:
                        for c4 in range(4):
                            CP(cpbc[:, c4, :], pv("cp", k + c4).to_broadcast([128, 128]))
                    MM(gp[:, :], cpbc[:, k % 4, :], wb[:, k, :], start=(k == 0), stop=False)
                MM(gp[:, :], ones_b[0:1, 0:128], rowb[:, gi * 512:(gi + 1) * 512], start=False, stop=True)
                for k in range(8):
                    MM(gs_[0:NS, :], cT17[:, k, 1:17], wb[:, k, :], start=(k == 0), stop=False)
                MM(gs_[0:NS, :], ones_b[0:1, 0:NS], rowb[:, gi * 512:(gi + 1) * 512], start=False, stop=True)
                ACT(GBC[:, gi * 512:(gi + 1) * 512], gp[:, :], AF.Copy, scale=0.5)
                ACT(GS[:, gi * 512:(gi + 1) * 512], gs_[0:NS, :], AF.Copy, scale=0.5)

        def build_diag():
            for c in range(8):
                for j in range(4):
                    TS(DG[:, c, j, :], ident, pv(f"wc{j}", c), None, ALU.mult)

        def phase_norm(x_d, ntile, rows, dstT, col0, smp):
            for i in range(ntile):
                xt = XT[0:rows, i % 2, :]
                DMA("sp", xt, x_d[i * 128:i * 128 + rows, :])
                ACT(XN[0:rows, i % 2, :], xt, AF.Square, accum=SSQ[0:rows, i:i + 1])
            n = ntile
            TS(SSQ[0:rows, 0:n], SSQ[0:rows, 0:n], 1024.0 * EPS, None, ALU.add)
            ACT(SSQ[0:rows, 20:20 + n], SSQ[0:rows, 0:n], AF.Ln)
            ACT(SSQ[0:rows, 20:20 + n], SSQ[0:rows, 20:20 + n], AF.Exp, scale=-0.5)
            for i in range(ntile):
                xt = XT[0:rows, i % 2, :]
                DMA("sp", xt, x_d[i * 128:i * 128 + rows, :])
                ACT(xt, xt, AF.Copy, scale=SSQ[0:rows, 20 + i:21 + i])
                for hlf in range(2):
                    pt = bank()
                    for c4 in range(4):
                        c = hlf * 4 + c4
                        idn = identF[0:rows, 0:rows]
                        o_ = pt[:, c4 * 128:c4 * 128 + rows]
                        i_ = xt[:, c * 128:(c + 1) * 128]
                        P.add("pe", (lambda o_=o_, i_=i_, idn=idn: (lambda e: e.transpose(out=o_, in_=i_, identity=idn)))(),
                              reads=[i_, idn], writes=[o_])
                    for c4 in range(4):
                        c = hlf * 4 + c4
                        src_ = pt[:, c4 * 128:c4 * 128 + rows]
                        if not smp:
                            TS(dstT[:, c, col0 + i * 128:col0 + i * 128 + rows], src_,
                               modS[:, 8 + c, 0:1], modS[:, c, 0:1], ALU.mult, ALU.add)
                        else:
                            t = tmp()
                            TT(t[:, 0:NS], src_, modS[:, 8 + c, 1:17], ALU.mult)
                            TT(dstT[:, c, col0:col0 + NS], t[:, 0:NS], modS[:, c, 1:17], ALU.add)

        def phase_norm_gen(x_d, ntile, rows, dstT, col0):
            for i in range(ntile):
                xt = XT[0:rows, i % 2, :]
                DMA("sp", xt, x_d[i * 128:i * 128 + rows, :])
                ACT(XN[0:rows, i % 2, :], xt, AF.Square, accum=SSQ[0:rows, i:i + 1])
                yield
            n = ntile
            TS(SSQ[0:rows, 0:n], SSQ[0:rows, 0:n], 1024.0 * EPS, None, ALU.add)
            ACT(SSQ[0:rows, 20:20 + n], SSQ[0:rows, 0:n], AF.Ln)
            ACT(SSQ[0:rows, 20:20 + n], SSQ[0:rows, 20:20 + n], AF.Exp, scale=-0.5)
            yield
            for i in range(ntile):
                xt = XT[0:rows, i % 2, :]
                DMA("sp", xt, x_d[i * 128:i * 128 + rows, :])
                ACT(xt, xt, AF.Copy, scale=SSQ[0:rows, 20 + i:21 + i])
                yield
                for hlf in range(2):
                    pt = banks[5 + hlf]
                    for c4 in range(4):
                        c = hlf * 4 + c4
                        idn = identF[0:rows, 0:rows]
                        o_ = pt[:, c4 * 128:c4 * 128 + rows]
                        i_ = xt[:, c * 128:(c + 1) * 128]
                        P.add("pe", (lambda o_=o_, i_=i_, idn=idn: (lambda e: e.transpose(out=o_, in_=i_, identity=idn)))(),
                              reads=[i_, idn], writes=[o_])
                    yield
                    for c4 in range(4):
                        c = hlf * 4 + c4
                        TS(dstT[:, c, col0 + i * 128:col0 + i * 128 + rows], pt[:, c4 * 128:c4 * 128 + rows],
                           modS[:, 8 + c, 0:1], modS[:, c, 0:1], ALU.mult, ALU.add)
                    yield

        hcars = [smalloc(1), smalloc(1)]
        RSs = sb("RSs", [128, 2, 2, NS], F32)

        def rnn_gen(c, src, h0ap, main, ch):
            RA, RI = (RARR_R if main else RARR_P)[ch]
            xrb = XRb[:, ch, :]
            hcar = hcars[ch]
            tcnt = [0]
            bcnt = [0]

            def T():
                t = TMP[:, ch * 4 + tcnt[0] % 4, :]
                tcnt[0] += 1
                return t

            def TB():
                t = TMPB[:, ch * 2 + bcnt[0] % 2, :]
                bcnt[0] += 1
                return t

            wx = wchunk(win_d, OFF_XR + c * 128)
            wg = wchunk(win_d, OFF_GR + c * 128) if main else None
            tiles = TILES if main else TILES[0:4]
            if main:
                TS(xrb[:, 1:4], XTL[:, c, :], pvf, None, ALU.mult)
            else:
                MSET(xrb[:, 1:4], 0.0)
            for (c0, n) in tiles:
                b = zmm(wx, src, c0, n)
                ACT(xrb[:, 4 + c0:4 + c0 + n], b[:, 0:n], AF.Copy)
                if main and n == NS:
                    CP(XRs[:, c, :], b[:, 0:NS])
                if main and c0 == 1536:
                    CP(CVP[:, c, :], b[:, 509:512])
                if (not main) and c0 == 1536:
                    CP(XTL[:, c, :], b[:, 509:512])
                yield
            hprev = h0ap
            batches = [tiles[0:2], tiles[2:]]
            for bt in batches:
                base = bt[0][0]
                for (c0, n) in bt:
                    smp = (n == NS)
                    lo = c0 - base
                    xc = T()
                    if not smp:
                        xcp = bank()
                        for j in range(4):
                            MM(xcp[:, 0:n], DG[:, c, j, :], xrb[:, c0 + 1 + j:c0 + 1 + j + n], start=(j == 0), stop=(j == 3))
                        yield
                        ACT(xc[:, 0:n], xcp[:, 0:n], AF.Identity, bias=pv("bcv", c))
                    else:
                        TS(xc[:, 0:n], XRs[:, c, :], pv("wc3", c), pv("bcv", c), ALU.mult, ALU.add)
                        for j in range(3):
                            STT(xc[:, 0:n], scT[:, c, j, :], pv(f"wc{j}", c), xc[:, 0:n], ALU.mult, ALU.add)
                        yield
                    yield
                    xcb = TB()
                    CP(xcb[:, 0:n], xc[:, 0:n])
                    yield
                    gi_ = bank()
                    MM(gi_[:, 0:n], wrab[:, 1, c, :], xcb[:, 0:n])
                    gr_ = bank()
                    MM(gr_[:, 0:n], wrab[:, 0, c, :], xcb[:, 0:n])
                    yield
                    t_i = T()
                    ACT(t_i[:, 0:n], gi_[:, 0:n], AF.Tanh, scale=0.5, bias=hbrx[:, c:c + 1])
                    ACT(RA[:, lo:lo + n], gr_[:, 0:n], AF.Tanh, scale=0.5, bias=hbra[:, c:c + 1])
                    yield
                    STT(RI[:, lo:lo + n], t_i[:, 0:n], 1.0, xc[:, 0:n], ALU.add, ALU.mult)
                    yield
                W01s = TMP[:, ch * 4:ch * 4 + 2, :].rearrange("p a n -> p (a n)")
                W23s = TMP[:, ch * 4 + 2:ch * 4 + 4, :].rearrange("p a n -> p (a n)")
                nb1 = sum(n for (_, n) in bt if n != NS)
                segs = [(0, nb1, W23s[:, 0:nb1], W01s[:, 0:nb1])]
                if any(n == NS for (_, n) in bt):
                    segs.append((nb1, NS, RSs[:, ch, 0, :], RSs[:, ch, 1, :]))
                for (o_, n_, wt, we) in segs:
                    ACT(wt, RA[:, o_:o_ + n_], AF.Tanh, scale=mhc2[:, c:c + 1], bias=mhc2[:, c:c + 1])
                    ACT(we, RA[:, o_:o_ + n_], AF.Exp, scale=hc[:, c:c + 1], bias=hc[:, c:c + 1])
                yield
                for (o_, n_, wt, we) in segs:
                    STT(wt, we, 1.0, wt, ALU.add, ALU.mult)
                yield
                for (o_, n_, wt, we) in segs:
                    TS(RA[:, o_:o_ + n_], wt, -1.0, 1.0, ALU.mult, ALU.add)
                yield
                W01 = TMP[:, ch * 4:ch * 4 + 2, :].rearrange("p a n -> p (a n)")
                W23 = TMP[:, ch * 4 + 2:ch * 4 + 4, :].rearrange("p a n -> p (a n)")
                nb = sum(n for (_, n) in bt if n != NS)
                has_s = any(n == NS for (_, n) in bt)
                ACT(W01[:, 0:nb], RA[:, 0:nb], AF.Square)
                if has_s:
                    ACT(W23[:, 0:NS], RA[:, nb:nb + NS], AF.Square)
                yield
                ACT(W01[:, 0:nb], W01[:, 0:nb], AF.Sqrt, scale=-1.0 / 16.0, bias=sixt)
                if has_s:
                    ACT(W23[:, 0:NS], W23[:, 0:NS], AF.Sqrt, scale=-1.0 / 16.0, bias=sixt)
                yield
                TT(RI[:, 0:nb], RI[:, 0:nb], W01[:, 0:nb], ALU.mult)
                if has_s:
                    TT(RI[:, nb:nb + NS], RI[:, nb:nb + NS], W23[:, 0:NS], ALU.mult)
                yield
                SCAN(W23[:, 0:nb], RA[:, 0:nb], RI[:, 0:nb], hprev)
                CP(hcar, W23[:, nb - 1:nb])
                hprev = hcar
                if base == 1024:
                    if main:
                        TS(HL[:, 8 + c:9 + c], W23[:, nb - 1:nb], 2.0, None, ALU.mult)
                    else:
                        TS(HL[:, c:c + 1], W23[:, nb - 1:nb], pvf, None, ALU.mult)
                yield
                if main:
                    k_ = 0
                    for (c0, n) in bt:
                        lo = c0 - base
                        if n == NS:
                            hs = W01[:, 0:NS]
                            TS(hs, shT[:, c, :], 0.5, None, ALU.mult)
                            TT(hs, hs, RA[:, lo:lo + n], ALU.mult)
                            TT(hs, hs, RI[:, lo:lo + n], ALU.add)
                            TS(HSs[:, c, :], hs, 2.0, None, ALU.mult)
                            tg = W01[:, 512:512 + NS]
                        else:
                            hs = W23[:, lo:lo + n]
                            tg = W01[:, k_ * 512:k_ * 512 + n]
                            k_ += 1
                        gb = zmm(wg, src, c0, n)
                        yield
                        ACT(tg, gb[:, 0:n], AF.Tanh, scale=0.5)
                        yield
                        STT(tg, tg, 1.0, gb[:, 0:n], ALU.add, ALU.mult)
                        TT(YR[:, c, c0:c0 + n], hs, tg, ALU.mult)
                        yield

        def run_pairs(gens, background=()):
            bg = list(background)
            for i in range(0, len(gens), 2):
                alive = list(gens[i:i + 2])
                while alive:
                    for g_ in list(alive):
                        try:
                            next(g_)
                        except StopIteration:
                            alive.remove(g_)
                    for g_ in list(bg):
                        try:
                            next(g_)
                        except StopIteration:
                            bg.remove(g_)
            for g_ in bg:
                for _ in g_:
                    pass

        _pool[:] = [0, 1, 2, 3, 4, 7]
        build_diag()
        phase_norm(xq_d, 16, 128, YA, 0, False)
        MSET(HL[:], 0.0)
        run_pairs([rnn_gen(c, YA, zeroc, False, c % 2) for c in range(8)],
                  background=[phase_norm_gen(xp_d, 16, 128, hT, 0)])

        P.mark("P")
        _pool[:] = [0, 1, 2, 3, 4, 5, 6, 7]
        eb = bank()
        extf = tmp()[0:16, 0:384]
        MM(eb[0:16, 0:383], relb[:, 0:16], relb[:, 16:399])
        ACT(extf[:, 0:383], eb[0:16, 0:383], AF.Copy, scale=8.0)
        P.mark("bias0")
        DMA("sp", ext_d[:, :], extf[:, 0:383])
        P.mark("bias1")
        for q8 in range(8):
            src = bass.AP(ext_d.tensor, q8 * 2 * 383, [[1, 128], [383, 2], [1, 256]])
            hk = tmp().rearrange("p (h q) -> p h q", h=2)
            DMA("sp", hk, src)
            P.mark("bias2")
            hkb = tmpb()
            CP(hkb[:, :], hk.rearrange("p h q -> p (h q)"))
            b = bank()
            P.mark("bias3")
            MM(b[:, :], Jm, hkb[:, :])
            hh = q8 * 2
            P.mark("bias4")
            CP(B8[:, hh:hh + 2, :].rearrange("p h q -> p (h q)"), b[:, :])
            P.mark("bias5")
            for h1 in range(2):
                TS(BS[:, hh + h1:hh + h1 + 1], b[:, h1 * 256 + 127:h1 * 256 + 128], 0.125, None, ALU.mult)

        P.mark("bias")
        phase_norm(xs_d, 1, NS, hT, NTOK, True)

        P.mark("MA")
        def qk_norm(b, n, gcol, dst):
            sq = tmpb()
            ACT(sq[:, 0:n], b[:, 0:n], AF.Square)
            sb_ = bank()
            MM(sb_[:, 0:n], bones, sq[:, 0:n])
            rl = tmp()
            ACT(rl[:, 0:n], sb_[:, 0:n], AF.Ln, bias=epsc)
            ACT(rl[:, 0:n], rl[:, 0:n], AF.Exp, scale=-0.5)
            STT(dst, b[:, 0:n], gcol, rl[:, 0:n], ALU.mult, ALU.mult)

        wkv = WB[:, 0, :, :]
        DMA("pool", wkv[:, :, 0:256], win_d[:, OFF_K:OFF_K + 256].rearrange("(c p) n -> p c n", p=128))
        DMA("pool", wkv[:, :, 256:512], win_d[:, OFF_V:OFF_V + 256].rearrange("(c p) n -> p c n", p=128))
        for blk in range(17):
            MSET(Vb[:, blk, :, 64:128], 1.0)

        def tokmajor(src, c0, rows, wcols):
            b = bank()
            for k in range(8):
                MM(b[0:rows, 0:256], src[:, k, c0:c0 + rows], wkv[:, k, wcols:wcols + 256], start=(k == 0),
                   stop=(k == 7))
            return b

        def emit_vblock(blk):
            if blk < 16:
                b = tokmajor(hT, blk * 128, 128, 256)
            else:
                b = tokmajor(YA, 1920, 128, 256)
            CP(Vb[:, blk, :, 0:64], b[:, 0:256].rearrange("p (g e) -> p g e", g=4))
            if blk == 15:
                CP(TK[:, 0, :], b[:, 0:256])
                DMA("sp", vwp_d[:, :], TK[:, 0, :], is_out=True)

        vi = 0
        for g in range(4):
            wk = wchunk(win_d, OFF_K + g * 64, dup=True)
            for (src_, c0, n, dst_) in [(hT, t0, tn, KT[:, g, t0:t0 + tn]) for (t0, tn) in TILES[0:4]] + \
                                       [(YA, 1920, 128, KT[:, g, 2064:2192])]:
                b = zmm(wk, src_, c0, n)
                if vi < 17:
                    emit_vblock(vi)
                    vi += 1
                qk_norm(b, n, pv("kg"), dst_)
        while vi < 17:
            emit_vblock(vi)
            vi += 1

        def tok_knorm(b, rows, dst):
            junk = TK[0:rows, 3, :]
            for g in range(4):
                ACT(junk[:, g * 64:(g + 1) * 64], b[0:rows, g * 64:(g + 1) * 64], AF.Square,
                    accum=SSQ[0:rows, 36 + g:37 + g])
            ACT(SSQ[0:rows, 36:40], SSQ[0:rows, 36:40], AF.Ln, scale=1.0 / 64.0, bias=epsc[0:rows, :])
            ACT(SSQ[0:rows, 36:40], SSQ[0:rows, 36:40], AF.Exp, scale=-0.5)
            for g in range(4):
                STT(dst[:, g * 64:(g + 1) * 64], b[0:rows, g * 64:(g + 1) * 64], SSQ[0:rows, 36 + g:37 + g],
                    kgbc[0:rows, :], ALU.mult, ALU.mult)

        b = tokmajor(hT, 1920, 128, 0)
        tok_knorm(b, 128, TK[:, 1, :])
        DMA("sp", kwp_d[:, :], TK[:, 1, :], is_out=True)
        b = tokmajor(hT, NTOK, NS, 0)
        tok_knorm(b, NS, TK[0:NS, 2, :])
        DMA("sp", knew_d[:, :], TK[0:NS, 2, :])
        DMA("sp", kws_d[:, 127, :], TK[0:NS, 2, :], is_out=True)
        b = tokmajor(hT, NTOK, NS, 256)
        CP(TK[0:NS, 0, :], b[0:NS, 0:256])
        DMA("sp", vnew_d[:, :], TK[0:NS, 0, :])
        DMA("sp", vws_d[:, 127, :], TK[0:NS, 0, :], is_out=True)
        DMA("sp", kws_d[:, 0:127, :], ck_d[:, 1:128, :], is_out=True)
        DMA("sp", vws_d[:, 0:127, :], cv_d[:, 1:128, :], is_out=True)
        DMA("sp", cvs12_d[:, :, :], sc12_d[:, :, :], is_out=True)

        P.mark("KV")
        _pool[:] = [0, 1, 2, 3]
        for j in range(8):
            g = j // 2
            qt = QT
            wga = wchunk(win_d, OFF_GA + j * 128)
            for (c0, n) in TILES:
                b = zmm(wga, hT, c0, n)
                e1 = tmp()
                ACT(e1[:, 0:n], b[:, 0:n], AF.Exp, scale=-1.0)
                ACT(e1[:, 0:n], e1[:, 0:n], AF.Ln, bias=onec)
                ACT(e1[:, 0:n], e1[:, 0:n], AF.Exp, scale=-1.0)
                if n == NS:
                    TT(SGS[:, j, :], e1[:, 0:n], b[:, 0:n], ALU.mult)
                else:
                    TT(SG[:, c0:c0 + n], e1[:, 0:n], b[:, 0:n], ALU.mult)
            wq = wchunk(win_d, OFF_Q + j * 128)
            for (c0, n) in TILES:
                b = zmm(wq, hT, c0, n)
                qk_norm(b, n, pv("qg"), qt[:, c0:c0 + n])
            CP(QTs[:, j, :], qt[:, NTOK:TW])
            def s_unit(m):
                kcol = 2064 if m < 0 else m * 128
                q0 = max(m, 0) * 128
                nq = 128 if (m < 0 or m == 15) else 256
                bo = 128 if m < 0 else 0
                sbk = bank()
                ebuf = EB[:, (m + 1) % 3, :, :]
                for hh in range(2):
                    rs_ = slice(hh * 64, hh * 64 + 64)
                    MM(sbk[:, hh * 256:hh * 256 + nq], KT[rs_, g, kcol:kcol + 128], qt[rs_, q0:q0 + nq],
                       start=True, stop=False)
                    MM(sbk[:, hh * 256:hh * 256 + nq], ident, B8[:, 2 * j + hh, bo:bo + nq], start=False, stop=True)
                for hh in range(2):
                    ACT(ebuf[:, hh, 0:nq], sbk[:, hh * 256:hh * 256 + nq], AF.Exp, scale=0.125,
                        bias=(pm if m < 0 else zeroc))

            def pv_unit(m):
                ebuf = EB[:, (m + 1) % 3, :, :]
                n_ = m
                ob = banks[4 + 2 * ((n_ // 4) % 2)]
                db = banks[5 + 2 * ((n_ // 4) % 2)]
                cs = slice((n_ % 4) * 128, (n_ % 4) * 128 + 128)
                eprev = EB[:, m % 3, :, :]
                pblk = 16 if n_ == 0 else n_ - 1
                pcol = slice(0, 128) if n_ == 0 else slice(128, 256)
                for hh in range(2):
                    os_ = slice(hh * 64, hh * 64 + 64)
                    tp = (0, 64 * hh)
                    MM(ob[os_, cs], Vb[:, pblk, g, 0:64], eprev[:, hh, pcol], start=True, stop=False, tp=tp)
                    MM(ob[os_, cs], Vb[:, n_, g, 0:64], ebuf[:, hh, 0:128], start=False, stop=True, tp=tp)
                    MM(db[os_, cs], Vb[:, pblk, g, 64:128], eprev[:, hh, pcol], start=True, stop=False, tp=tp)
                    MM(db[os_, cs], Vb[:, n_, g, 64:128], ebuf[:, hh, 0:128], start=False, stop=True, tp=tp)
                if n_ % 4 == 3:
                    tt_ = n_ // 4
                    ld = tmp()
                    ACT(ld[:, :], db[:, :], AF.Ln, bias=esk[:, j:j + 1])
                    ACT(ld[:, :], ld[:, :], AF.Exp, scale=-1.0)
                    TT(ld[:, :], ld[:, :], SG[:, tt_ * 512:(tt_ + 1) * 512], ALU.mult)
                    TT(YA[:, j, tt_ * 512:(tt_ + 1) * 512], ob[:, :], ld[:, :], ALU.mult)

            s_unit(-1)
            s_unit(0)
            for m in range(0, 16):
                if m + 1 <= 15:
                    s_unit(m + 1)
                pv_unit(m)

        _pool[:] = [0, 1, 2, 3, 4, 7]
        Keff = KV[:, 0:4096].rearrange("p (b f) -> p b f", b=NS)
        Veff = KV[:, 4096:8192].rearrange("p (b f) -> p b f", b=NS)
        for bsm in range(NS):
            DMA("pool", Keff[0:127, bsm, :], ck_d[bsm, 1:128, :])
            DMA("pool", Veff[0:127, bsm, :], cv_d[bsm, 1:128, :])
            DMA("pool", Keff[127:128, bsm, :], knew_d[bsm:bsm + 1, :])
            DMA("pool", Veff[127:128, bsm, :], vnew_d[bsm:bsm + 1, :])
        P.mark("sattdma")
        ssb = banks[5]
        for bsm in range(NS):
            ktp = bank()
            for g in range(4):
                for hh in range(2):
                    MM(ktp[hh * 64:hh * 64 + 64, g * 128:(g + 1) * 128], Keff[:, bsm, g * 64:(g + 1) * 64], ident,
                       tp=(0, 64 * hh))
            kts = TMPB[:, bsm % 2, :]
            CP(kts, ktp[:, :])
            for g in range(4):
                for hh in range(2):
                    rs_ = slice(hh * 64, hh * 64 + 64)
                    c_ = bsm * 16 + 4 * g + hh
                    MM(ssb[:, c_:c_ + 3:2], kts[rs_, g * 128:(g + 1) * 128], QTs[rs_, 2 * g:2 * g + 2, bsm])
        P.mark("satts")
        ssv = ssb[:, 0:256].rearrange("p (b h) -> p b h", h=16)
        for h in range(16):
            ACT(ESs[:, :, h], ssv[:, :, h], AF.Exp, scale=0.125, bias=BS[:, h:h + 1])
        osb = banks[6]
        for bsm in range(NS):
            for g in range(4):
                for hh in range(2):
                    os_ = slice(hh * 64, hh * 64 + 64)
                    c_ = (2 * g) * 16 + bsm
                    MM(osb[os_, c_:c_ + 17:16], Veff[:, bsm, g * 64:(g + 1) * 64],
                       ESs[:, bsm, 4 * g + hh:4 * g + hh + 3:2], tp=(0, 64 * hh))
        dsb = bank()
        MM(dsb[:, 0:256], ones_b[:], ESs[:, :, :].rearrange("p b h -> p (b h)"))
        dsv = dsb[:, 0:256].rearrange("p (b h) -> p b h", h=16)
        for j in range(8):
            for hh in range(2):
                os_ = slice(hh * 64, hh * 64 + 64)
                CP(DSs[os_, j, :], dsv[os_, :, 2 * j + hh])
            ACT(DSs[:, j, :], DSs[:, j, :], AF.Ln, bias=esk[:, j:j + 1])
            ACT(DSs[:, j, :], DSs[:, j, :], AF.Exp, scale=-1.0)
            TT(DSs[:, j, :], DSs[:, j, :], SGS[:, j, :], ALU.mult)
            TT(YA[:, j, NTOK:TW], osb[:, j * 16:(j + 1) * 16], DSs[:, j, :], ALU.mult)

        P.mark("satt")
        _pool[:] = [0, 1, 2, 3, 4, 7]
        build_diag()
        run_pairs([rnn_gen(c, hT, HL[:, c:c + 1], True, c % 2) for c in range(8)])
        DMA("sp", hp_d[:, :], HL[:, 8:16], is_out=True)
        DMA("sp", cvp_d[:, :, :], CVP[:], is_out=True)
        DMA("sp", cvs_d[:, :, :], XRs[:], is_out=True)
        DMA("sp", hsm_d[:, :, :], HSs[:], is_out=True)

        P.mark("R")
        build_gates()
        for hlf in range(2):
            DMA("pool", WB[:, hlf, :, :], wout_d[:, hlf * 512:(hlf + 1) * 512].rearrange("(c p) n -> p c n", p=128))
        MGs = sb("MGs", [128, 8, NS], BF16)

        def g_elem(e, c0, n, wma, wmr, wpa, wpr, dst):
            bm = zmm(wma, hT, c0, n)
            tm = tmp()
            ACT(tm[:, 0:n], bm[:, 0:n], AF.Tanh, scale=0.5)
            br = zmm(wmr, hT, c0, n)
            tr_ = tmp()
            ACT(tr_[:, 0:n], br[:, 0:n], AF.Tanh, scale=0.5)
            bpa = zmm(wpa, YA, c0, n)
            STT(tm[:, 0:n], tm[:, 0:n], 1.0, bpa[:, 0:n], ALU.add, ALU.mult)
            bpr = zmm(wpr, YR, c0, n)
            STT(tr_[:, 0:n], tr_[:, 0:n], 1.0, bpr[:, 0:n], ALU.add, ALU.mult)
            TT(dst[:, e, 0:n], tm[:, 0:n], tr_[:, 0:n], ALU.add)

        def g_load(x_src, n, bi):
            rows = min(128, n - bi * 128)
            DMA("sp", XT[0:rows, bi % 2, :], x_src[bi * 128:bi * 128 + rows, :])

        def g_final(src, n, x_src, y_dst, gsel):
            nblk = (n + 127) // 128
            for bi in range(nblk):
                rows = min(128, n - bi * 128)
                xt = XT[0:rows, bi % 2, :]
                for hlf in range(2):
                    b = bank()
                    for e in range(8):
                        MM(b[0:rows, :], src[:, e, bi * 128:bi * 128 + rows], WB[:, hlf, e, :], start=(e == 0),
                           stop=(e == 7))
                    t = tmp()
                    gsrc = GS[:, hlf * 512:(hlf + 1) * 512] if gsel else GBC[:, hlf * 512:(hlf + 1) * 512]
                    TT(t[0:rows, :], b[0:rows, :], gsrc, ALU.mult)
                    TT(xt[:, hlf * 512:(hlf + 1) * 512], t[0:rows, :], xt[:, hlf * 512:(hlf + 1) * 512], ALU.add)
                DMA("sp", y_dst[bi * 128:bi * 128 + rows, :], xt, is_out=True)
                if bi + 2 < nblk:
                    g_load(x_src, n, bi + 2)

        WS2 = LOCb[:, 4096:8192].rearrange("p (s c n) -> p s c n", s=4, c=8)

        def gchunk(src_d, col0, pool_, slot):
            w = pool_[:, slot, :, :]
            DMA("pool", w, src_d[:, col0:col0 + 128].rearrange("(c p) n -> p c n", p=128))
            return w

        for ti, (c0, n) in enumerate(TILES[0:4]):
            g_load(xp_d[c0:c0 + n, :], n, 0)
            g_load(xp_d[c0:c0 + n, :], n, 1)
            for e in range(8):
                pool_ = WS if e % 2 == 0 else WS2
                wma = gchunk(win_d, OFF_MA + e * 128, pool_, 0)
                wmr = gchunk(win_d, OFF_MR + e * 128, pool_, 1)
                wpa = gchunk(wpa_d, e * 128, pool_, 2)
                wpr = gchunk(wpr_d, e * 128, pool_, 3)
                g_elem(e, c0, n, wma, wmr, wpa, wpr, MGt)
                if ti == 3:
                    g_elem(e, NTOK, NS, wma, wmr, wpa, wpr, MGs)
            g_final(MGt, n, xp_d[c0:c0 + n, :], y_d[c0:c0 + n, :], False)
        g_load(xs_d, NS, 0)
        g_final(MGs, NS, xs_d, ys_d, True)

        with ExitStack() as es2:
            sems = {e: es2.enter_context(nc.semaphore(f"ksem_{e}")) for e in ("pe", "act", "dve", "pool")}
            dma_sems = {"sp": [es2.enter_context(nc.semaphore(f"d_sp{i}")) for i in range(24)],
                        "pool": [es2.enter_context(nc.semaphore(f"d_pl{i}")) for i in range(12)]}
            run = P.emit_all(sems, dma_sems)
            with nc.Block() as block:
                @block.tensor
                def _(e):
                    run("pe", e)

                @block.scalar
                def _(e):
                    run("act", e)

                @block.vector
                def _(e):
                    run("dve", e)

                @block.gpsimd
                def _(e):
                    run("pool", e)

                @block.sync
                def _(e):
                    run("sp", e)
    return nc


def _fm(v):
    return np.ascontiguousarray(np.asarray(v, np.float32).reshape(8, 128).T)


_NC_CACHE = {}


def kernel(x_prompt, x_sample, cache_k, cache_v, state_conv, state_h, c_prompt, c_sample, rel_table, norm_g,
           w_ada, b_ada, w_in, q_norm_g, k_norm_g, sinks, w_conv, b_conv, w_rg_a, b_rg_a, w_rg_x, b_rg_x,
           lru_lambda, w_proj_attn, w_proj_rnn, w_out):
    f32 = np.float32
    x_prompt = np.asarray(x_prompt, f32)
    x_sample = np.asarray(x_sample, f32)
    ident = np.eye(128, dtype=f32)
    Jm = ident[::-1].copy()
    bones = np.zeros((128, 128), f32)
    bones[0:64, 0:64] = 1.0 / 64.0
    bones[64:128, 64:128] = 1.0 / 64.0
    cst = np.ascontiguousarray(np.stack([ident, Jm, bones], axis=1))
    import math
    ohe = np.zeros((33, 383), f32)
    for dd in range(128):
        nn = max(dd, 0)
        if nn < 16:
            bkt = nn
        else:
            nf = np.float32(max(nn, 1))
            bkt = 16 + int(np.int32(np.log(nf / np.float32(16)) / np.float32(math.log(128 / 16)) * np.float32(16)))
            bkt = min(bkt, 31)
        ohe[bkt, 127 + dd] = 1.0
    ohe[32, :] = NEGB
    ohe[32, 127:255] = 0.0
    relx = np.concatenate([np.asarray(rel_table, f32), np.ones((1, 16), f32)], axis=0)

    def blockdiag(w):
        w = np.asarray(w, f32)[0]
        o = np.zeros((128, 8, 128), f32)
        for c in range(8):
            for nl in range(2):
                o[nl * 64:(nl + 1) * 64, c, nl * 64:(nl + 1) * 64] = w[2 * c + nl]
        return o

    wra = blockdiag(w_rg_a)
    wrx = blockdiag(w_rg_x)
    b_ada0 = np.asarray(b_ada, f32)[0]
    kgbc = np.ascontiguousarray(np.broadcast_to(np.asarray(k_norm_g, f32)[0][None, :], (128, 64)))
    common = dict(w_ada=np.asarray(w_ada, f32)[0], w_in=np.asarray(w_in, f32)[0],
                  w_pa=np.asarray(w_proj_attn, f32)[0], w_pr=np.asarray(w_proj_rnn, f32)[0],
                  w_out=np.asarray(w_out, f32)[0], wra=wra, wrx=wrx, cst=cst, ohe=ohe, relx=relx,
                  bgate=np.ascontiguousarray(b_ada0[2048:3072][None, :]), kgbc=kgbc)
    in_maps = []
    for r in range(8):
        s, hf = r // 2, r % 2
        pvec = np.zeros((128, NPV), f32)

        def put(name, arr):
            o, w = PV[name]
            pvec[:, o:o + w] = arr.reshape(128, w)

        put("ng", _fm(np.asarray(norm_g, f32)[0]))
        put("bsh", _fm(b_ada0[0:1024]))
        put("bsc", _fm(b_ada0[1024:2048]))
        for j in range(4):
            put(f"wc{j}", _fm(np.asarray(w_conv, f32)[0, j]))
        put("bcv", _fm(np.asarray(b_conv, f32)[0]))
        put("bra", _fm(np.asarray(b_rg_a, f32)[0]))
        put("brx", _fm(np.asarray(b_rg_x, f32)[0]))
        put("lam", _fm(np.asarray(lru_lambda, f32)[0]))
        put("qg", np.tile(np.asarray(q_norm_g, f32)[0], 2)[:, None])
        put("kg", np.tile(np.asarray(k_norm_g, f32)[0], 2)[:, None])
        put("snk", np.repeat(np.asarray(sinks, f32)[0].reshape(8, 2), 64, axis=1).T)
        put("cp", _fm(np.asarray(c_prompt, f32)[s]))
        put("pv", np.full((128, 1), 1.0 if hf else 0.0, f32))
        put("pm", np.full((128, 1), 0.0 if hf else -1e30, f32))
        bs = slice(r * NS, (r + 1) * NS)
        cs = np.asarray(c_sample, f32)[bs]
        csT = np.ascontiguousarray(cs.T.reshape(8, 128, NS).transpose(1, 0, 2))
        sc = np.asarray(state_conv, f32)[0, bs]
        scT = np.ascontiguousarray(sc.transpose(2, 1, 0).reshape(8, 128, 3, NS).transpose(1, 0, 2, 3))
        sh = np.asarray(state_h, f32)[0, bs]
        shT = np.ascontiguousarray(sh.T.reshape(8, 128, NS).transpose(1, 0, 2))
        m = dict(common)
        m.update(xp=np.ascontiguousarray(x_prompt[s, hf * NTOK:(hf + 1) * NTOK]),
                 xq=(np.ascontiguousarray(x_prompt[s, 0:NTOK]) if hf else np.zeros((NTOK, D), f32)),
                 xs=np.ascontiguousarray(x_sample[bs, 0]),
                 ck=np.ascontiguousarray(np.asarray(cache_k, f32)[0, bs].reshape(NS, 128, 256)),
                 cv=np.ascontiguousarray(np.asarray(cache_v, f32)[0, bs].reshape(NS, 128, 256)),
                 scT=scT, shT=shT, sc12=np.ascontiguousarray(sc[:, 1:3, :]), pvec=pvec, csT=csT)
        in_maps.append(m)
    if "nc" not in _NC_CACHE:
        _NC_CACHE["nc"] = build_nc()
    nc = _NC_CACHE["nc"]
    res = run_bass_kernel_spmd(nc, in_maps, core_ids=list(range(8)))
    R = res.results

    y_p = np.zeros((4, 4096, D), f32)
    y_s = np.zeros((128, 1, D), f32)
    kwp = np.zeros((1, 4, 128, 4, 64), f32)
    vwp = np.zeros((1, 4, 128, 4, 64), f32)
    cvp = np.zeros((1, 4, 3, D), f32)
    hp = np.zeros((1, 4, D), f32)
    kws = np.zeros((1, 128, 128, 4, 64), f32)
    vws = np.zeros((1, 128, 128, 4, 64), f32)
    cvs = np.zeros((1, 128, 3, D), f32)
    hsm = np.zeros((1, 128, D), f32)
    for r in range(8):
        s, hf = r // 2, r % 2
        o = R[r]
        y_p[s, hf * NTOK:(hf + 1) * NTOK] = o["y"]
        bs = slice(r * NS, (r + 1) * NS)
        y_s[bs, 0] = o["ys"]
        if hf:
            kwp[0, s] = o["kwp"].reshape(128, 4, 64)
            vwp[0, s] = o["vwp"].reshape(128, 4, 64)
            cvp[0, s] = o["cvp"].transpose(2, 1, 0).reshape(3, D)
            hp[0, s] = o["hp"].T.reshape(D)
        kws[0, bs] = o["kws"].reshape(NS, 128, 4, 64)
        vws[0, bs] = o["vws"].reshape(NS, 128, 4, 64)
        cvs[0, bs, 0:2] = o["cvs12"]
        cvs[0, bs, 2] = o["cvs"].transpose(2, 1, 0).reshape(NS, D)
        hsm[0, bs] = o["hsm"].transpose(2, 1, 0).reshape(NS, D)
    return (y_p, y_s, kwp, vwp, cvp, hp, kws, vws, cvs, hsm)
```
